# Optimizing a Trainium2 kernel written in Bass

```python
import math
import jax, jax.numpy as jnp
from jax import lax
import numpy as np


D_MODEL = 1024
BATCH = 4
SEQ = 8192
DEPTH = 1

GRID_W = 64
CTX_LEN = 256
N_DIFF_HEADS = 8
HEAD_DIM = 64
V_HEAD_DIM = 2 * HEAD_DIM
QK_W = 2 * N_DIFF_HEADS * HEAD_DIM
ATTN_W = N_DIFF_HEADS * V_HEAD_DIM
CONV_W = D_MODEL
CONV_K = 31
D_FF = 2816
N_MOD = 9
ROPE_BASE = 10000.0
AXIS_DIM = HEAD_DIM // 2
Q_BLOCK = 128
EPS = 1e-6
IN_SPLITS = (QK_W, 2 * QK_W, 2 * QK_W + ATTN_W, 2 * QK_W + ATTN_W + 2 * CONV_W,
             2 * QK_W + ATTN_W + 2 * CONV_W + D_MODEL)
IN_W = 2 * QK_W + ATTN_W + 2 * CONV_W + 2 * D_MODEL

kernel_name = 'hybrid_diffattn_conformer_macaron_dit_block'


def _rms(x, g):
    xf = x.astype(jnp.float32)
    y = xf * lax.rsqrt(jnp.mean(xf * xf, axis=-1, keepdims=True) + EPS)
    return (y * g.astype(jnp.float32)).astype(x.dtype)


def _layernorm(x, g, b):
    xf = x.astype(jnp.float32)
    mu = jnp.mean(xf, axis=-1, keepdims=True)
    xc = xf - mu
    var = jnp.mean(xc * xc, axis=-1, keepdims=True)
    y = xc * lax.rsqrt(var + EPS) * g.astype(jnp.float32) + b.astype(jnp.float32)
    return y.astype(x.dtype)


def _sub_in(x, g_pre, m, k):
    return _rms(x, g_pre) * (1 + m[k + 1]) + m[k]


def _sub_out(x, y, g_post, m, k, mult):
    return x + mult * m[k + 2] * _rms(y, g_post)


def _swiglu(h, w13, w2):
    gate, up = jnp.split(h @ w13, 2, axis=-1)
    return (jax.nn.silu(gate) * up) @ w2


def _rope2d(x, cos, sin):
    xs = x.reshape(x.shape[:-1] + (2, 2, AXIS_DIM // 2))
    x1, x2 = xs[..., 0, :], xs[..., 1, :]
    c = cos[None, :, None]
    s = sin[None, :, None]
    out = jnp.stack([x1 * c - x2 * s, x2 * c + x1 * s], axis=-2)
    return out.reshape(x.shape).astype(x.dtype)


def _diff_attn(q, k, v, lam):
    s = jnp.einsum('bqgd,bkgd->bgqk', q, k).astype(jnp.float32) * (HEAD_DIM ** -0.5)
    p = jax.nn.softmax(s, axis=-1)
    bsz, _, nq, nk = p.shape
    p = p.reshape(bsz, N_DIFF_HEADS, 2, nq, nk)
    a = p[:, :, 0] - lam * p[:, :, 1]
    return jnp.einsum('bhqk,bkhe->bqhe', a.astype(v.dtype), v)


def _diff_attn_blocked(q, k, v, lam):
    bsz, n, g, d = q.shape
    nb = n // Q_BLOCK
    qb = q.reshape(bsz, nb, Q_BLOCK, g, d).transpose(1, 0, 2, 3, 4)
    ob = lax.map(lambda blk: _diff_attn(blk, k, v, lam), qb)
    return ob.transpose(1, 0, 2, 3, 4).reshape(bsz, n, N_DIFF_HEADS, V_HEAD_DIM)


def _diff_heads_out(o, subln_g, lam_init):
    o = _rms(o, subln_g) * (1 - lam_init)
    return o.reshape(o.shape[:2] + (ATTN_W,))


def _conformer_conv(glu_in, w, b, ln_g, ln_b, w_proj):
    a, gt = jnp.split(glu_in, 2, axis=-1)
    u = a * jax.nn.sigmoid(gt)
    u = lax.conv_general_dilated(u, w[:, None, :].astype(u.dtype), window_strides=(1,),
                                 padding=[(CONV_K // 2, CONV_K // 2)],
                                 dimension_numbers=('NWC', 'WIO', 'NWC'),
                                 feature_group_count=CONV_W) + b
    u = jax.nn.silu(_layernorm(u, ln_g, ln_b))
    return u @ w_proj


def _merge(o_attn, glu_in, ga, gb, p):
    y_attn = o_attn @ p['w_attn_proj']
    y_conv = _conformer_conv(glu_in, p['conv_w'], p['conv_b'], p['conv_ln_g'], p['conv_ln_b'],
                             p['w_conv_proj'])
    return (jax.nn.sigmoid(ga) * y_attn + jax.nn.sigmoid(gb) * y_conv) @ p['w_out']


def _layer(x, xc, m_lat, m_ctx, cos, sin, p, lam_init, ctx_out):
    g = p['norm_g']
    bsz, n = x.shape[:2]
    nc = xc.shape[1]
    H2 = 2 * N_DIFF_HEADS

    def ffn_step(xx, m, k, w13, w2, gi):
        y = _swiglu(_sub_in(xx, g[gi], m, k), w13, w2)
        return _sub_out(xx, y, g[gi + 1], m, k, 0.5)

    x = ffn_step(x, m_lat, 0, p['ffn1_w13'], p['ffn1_w2'], 0)
    xc = ffn_step(xc, m_ctx, 0, p['ffn1_w13'], p['ffn1_w2'], 0)

    lq = p['lambda_qk'].astype(jnp.float32)
    lam = jnp.exp(jnp.sum(lq[0] * lq[1])) - jnp.exp(jnp.sum(lq[2] * lq[3])) + lam_init

    h = _sub_in(x, g[2], m_lat, 3)
    q, k, v, glu_in, ga, gb = jnp.split(h @ p['w_in'], IN_SPLITS, axis=-1)
    hc = _sub_in(xc, g[2], m_ctx, 3)
    if ctx_out:
        qc, kc, vc, glu_c, gac, gbc = jnp.split(hc @ p['w_in'], IN_SPLITS, axis=-1)
    else:
        kc, vc = jnp.split(hc @ p['w_in'][:, QK_W:IN_SPLITS[2]], [QK_W], axis=-1)
    kc = kc.reshape(bsz, nc, H2, HEAD_DIM)
    vc = vc.reshape(bsz, nc, N_DIFF_HEADS, V_HEAD_DIM)

    q = _rope2d(q.reshape(bsz, n, H2, HEAD_DIM), cos, sin)
    k = _rope2d(k.reshape(bsz, n, H2, HEAD_DIM), cos, sin)
    v = v.reshape(bsz, n, N_DIFF_HEADS, V_HEAD_DIM)
    k_all = jnp.concatenate([kc, k], axis=1)
    v_all = jnp.concatenate([vc, v], axis=1)
    o_attn = _diff_heads_out(_diff_attn_blocked(q, k_all, v_all, lam), p['subln_g'], lam_init)
    y = _merge(o_attn, glu_in, ga, gb, p)
    x = _sub_out(x, y, g[3], m_lat, 3, 1.0)

    x = ffn_step(x, m_lat, 6, p['ffn2_w13'], p['ffn2_w2'], 4)

    if ctx_out:
        qc = qc.reshape(bsz, nc, H2, HEAD_DIM)
        oc_attn = _diff_heads_out(_diff_attn(qc, kc, vc, lam), p['subln_g'], lam_init)
        yc = _merge(oc_attn, glu_c, gac, gbc, p)
        xc = _sub_out(xc, yc, g[3], m_ctx, 3, 1.0)
        xc = ffn_step(xc, m_ctx, 6, p['ffn2_w13'], p['ffn2_w2'], 4)
    return x, xc


def setup_inputs(seed: int = 0) -> dict:
    key = jax.random.key(seed)
    ks = jax.random.split(key, 21)
    f32 = jnp.float32
    D = D_MODEL

    def nrm(k, shape, scale):
        return jax.random.normal(k, shape, f32) * scale

    return {
        'x': nrm(ks[0], (BATCH, SEQ, D), 1.0),
        'c': nrm(ks[1], (BATCH, D), 1.0),
        'ctx': nrm(ks[2], (BATCH, CTX_LEN, D), 1.0),
        'c_ctx': nrm(ks[3], (D,), 1.0),
        'w_ada': nrm(ks[4], (DEPTH, D, N_MOD * D), D ** -0.5),
        'b_ada': nrm(ks[5], (DEPTH, N_MOD * D), 0.02),
        'norm_g': 1.0 + nrm(ks[6], (DEPTH, 6, D), 0.05),
        'ffn1_w13': nrm(ks[7], (DEPTH, D, 2 * D_FF), D ** -0.5),
        'ffn1_w2': nrm(ks[8], (DEPTH, D_FF, D), D_FF ** -0.5),
        'ffn2_w13': nrm(ks[9], (DEPTH, D, 2 * D_FF), D ** -0.5),
        'ffn2_w2': nrm(ks[10], (DEPTH, D_FF, D), D_FF ** -0.5),
        'w_in': nrm(ks[11], (DEPTH, D, IN_W), D ** -0.5),
        'lambda_qk': nrm(ks[12], (DEPTH, 4, HEAD_DIM), 0.1),
        'subln_g': 1.0 + nrm(ks[13], (DEPTH, V_HEAD_DIM), 0.05),
        'w_attn_proj': nrm(ks[14], (DEPTH, ATTN_W, D), ATTN_W ** -0.5),
        'conv_w': nrm(ks[15], (DEPTH, CONV_K, CONV_W), CONV_K ** -0.5),
        'conv_b': nrm(ks[16], (DEPTH, CONV_W), 0.02),
        'conv_ln_g': 1.0 + nrm(ks[17], (DEPTH, CONV_W), 0.05),
        'conv_ln_b': nrm(ks[18], (DEPTH, CONV_W), 0.02),
        'w_conv_proj': nrm(ks[19], (DEPTH, CONV_W, D), CONV_W ** -0.5),
        'w_out': nrm(ks[20], (DEPTH, D, D), D ** -0.5),
    }


def reference(x, c, ctx, c_ctx, w_ada, b_ada, norm_g, ffn1_w13, ffn1_w2, ffn2_w13, ffn2_w2,
              w_in, lambda_qk, subln_g, w_attn_proj, conv_w, conv_b, conv_ln_g, conv_ln_b,
              w_conv_proj, w_out):
    bsz, n_lat, d = x.shape
    n_rows = n_lat // GRID_W
    row = jnp.repeat(jnp.arange(n_rows, dtype=jnp.int32), GRID_W)
    col = jnp.tile(jnp.arange(GRID_W, dtype=jnp.int32), n_rows)
    inv_freq = ROPE_BASE ** (-jnp.arange(0, AXIS_DIM, 2, dtype=jnp.float32) / AXIS_DIM)
    ang = jnp.stack([row, col], axis=-1).astype(jnp.float32)[..., None] * inv_freq
    cos, sin = jnp.cos(ang), jnp.sin(ang)

    xc = ctx
    for i in range(DEPTH):
        m_lat = (jax.nn.silu(c) @ w_ada[i] + b_ada[i]).reshape(bsz, N_MOD, 1, d).transpose(1, 0, 2, 3)
        m_ctx = (jax.nn.silu(c_ctx) @ w_ada[i] + b_ada[i]).reshape(N_MOD, 1, 1, d)
        p = {'norm_g': norm_g[i], 'ffn1_w13': ffn1_w13[i], 'ffn1_w2': ffn1_w2[i],
             'ffn2_w13': ffn2_w13[i], 'ffn2_w2': ffn2_w2[i], 'w_in': w_in[i],
             'lambda_qk': lambda_qk[i], 'subln_g': subln_g[i], 'w_attn_proj': w_attn_proj[i],
             'conv_w': conv_w[i], 'conv_b': conv_b[i], 'conv_ln_g': conv_ln_g[i],
             'conv_ln_b': conv_ln_b[i], 'w_conv_proj': w_conv_proj[i], 'w_out': w_out[i]}
        lam_init = 0.8 - 0.6 * math.exp(-0.3 * i)
        x, xc = _layer(x, xc, m_lat, m_ctx, cos, sin, p, lam_init, i < DEPTH - 1)
    return x
```

```python
import math
from contextlib import ExitStack
import numpy as np
import concourse.bass as bass
import concourse.mybir as mybir
from concourse.bass_utils import run_bass_kernel_spmd

F32 = mybir.dt.float32
BF16 = mybir.dt.bfloat16
AF = mybir.ActivationFunctionType
ALU = mybir.AluOpType

D = 1024
DFF = 2816
NF = DFF // 128
CTX = 256
CONV_K = 31
EPS = 1e-6
LAM_INIT = 0.2
GRID_W = 64


class Trk:
    def __init__(self, nc, es):
        self.nc = nc
        self.es = es
        self.E = {'pe': nc.tensor, 'act': nc.scalar, 'dve': nc.vector, 'pool': nc.gpsimd, 'sp': nc.sync}
        self.sem = {k: es.enter_context(nc.semaphore('s_' + k)) for k in ['pe', 'act', 'dve', 'pool']}
        self.cnt = {k: 0 for k in self.sem}
        self.dsem = {}
        self.waited = {e: {} for e in self.E}
        self.res = {}

    def _semobj(self, sk):
        return self.sem[sk] if sk in self.sem else self.dsem[sk][0]

    def _need(self, reads, writes):
        best = {}

        def add(sk, v):
            if v > best.get(sk, 0):
                best[sk] = v
        for r in reads:
            e = self.res.get(r)
            if e and e[0]:
                add(*e[0])
        for w in writes:
            e = self.res.get(w)
            if e:
                if e[0]:
                    add(*e[0])
                for sk, v in e[1].items():
                    add(sk, v)
        return best

    def _wait(self, eng, best):
        for sk, v in best.items():
            if eng == 'pe' and sk == 'pe':
                continue
            if self.waited[eng].get(sk, 0) >= v:
                continue
            self.E[eng].wait_ge(self._semobj(sk), v)
            self.waited[eng][sk] = v

    def _update(self, tok, reads, writes):
        for r in reads:
            e = self.res.setdefault(r, [None, {}])
            if tok[1] > e[1].get(tok[0], 0):
                e[1][tok[0]] = tok[1]
        for w in writes:
            self.res[w] = [tok, {}]

    def op(self, eng, reads, writes, fn):
        self._wait(eng, self._need(reads, writes))
        ins = fn()
        self.cnt[eng] += 1
        ins.then_inc(self.sem[eng], 1)
        self._update((eng, self.cnt[eng]), reads, writes)

    def dma(self, q, semname, reads, writes, pairs):
        if semname not in self.dsem:
            self.dsem[semname] = [self.es.enter_context(self.nc.semaphore('d_' + semname)), 0]
        self._wait(q, self._need(reads, writes))
        d = self.dsem[semname]
        for (o, i) in pairs:
            self.E[q].dma_start(out=o, in_=i).then_inc(d[0], 16)
            d[1] += 16
        self._update((semname, d[1]), reads, writes)

    def wait_all(self, eng, keys):
        self._wait(eng, self._need(keys, keys))


def build(NO, dbg=False):
    NALL = 2 * NO + CTX
    NKT = NALL // 128
    NTO = NO // 512
    nc = bass.Bass("TRN2", target_bir_lowering=False)

    def din(name, shape, dt=F32):
        return nc.dram_tensor(name, shape, dt, kind="ExternalInput").ap()

    def dscr(name, shape, dt):
        return nc.dram_tensor(name, shape, dt, kind="ExternalOutput" if dbg else "Internal").ap()

    xc = din("xc", [NALL, D])
    cvec = din("cvec", [128, 16])
    w_ada = din("w_ada", [D, 9 * D])
    b_ada = din("b_ada", [1, 9 * D])
    norm_g = din("norm_g", [6, D])
    w13 = [din("ffn1_w13", [D, 2 * DFF]), din("ffn2_w13", [D, 2 * DFF])]
    w2 = [din("ffn1_w2", [DFF, D]), din("ffn2_w2", [DFF, D])]
    w_in = din("w_in", [D, 7 * D])
    lam_qk = din("lambda_qk", [1, 256])
    colv = din("colv", [128, 40])
    conv_w = din("conv_w", [128, 8, CONV_K])
    w_ap = din("w_attn_proj", [D, D])
    w_cp = din("w_conv_proj", [D, D])
    w_o = din("w_out", [D, D])
    cosT = din("cosT", [128, NALL])
    sinT = din("sinT", [128, NALL])
    out = nc.dram_tensor("out", [NO, D], F32, kind="ExternalOutput").ap()

    TAB = dscr("TAB", [14, 128, D], F32)
    X1 = dscr("X1", [NALL, D], F32)
    X2 = dscr("X2", [NO, D], F32)
    KT = dscr("KT", [8, 128, NALL], BF16)
    QT = dscr("QT", [8, 128, NO], BF16)
    V = dscr("V", [NALL, D], BF16)
    U = dscr("U", [128, 8, NO + 1024], BF16)
    SGA = dscr("SGA", [128, 8, NO], BF16)
    SGB = dscr("SGB", [128, 8, NO], BF16)
    OT = dscr("OT", [128, 8, NO], BF16)

    kc = lambda w: w.rearrange("(c p) n -> p c n", p=128)

    with ExitStack() as top:
        T = Trk(nc, top)
        _uid = [0]

        def sbt(es, n, s, d):
            _uid[0] += 1
            return es.enter_context(nc.sbuf_tensor(f"{n}_{_uid[0]}", s, d))
        ident = sbt(top, "ident", [128, 128], BF16)
        onesb = sbt(top, "onesb", [128, 128], BF16)
        onesf = sbt(top, "onesf", [128, 128], F32)
        epsc = sbt(top, "epsc", [128, 1], F32)
        cols = sbt(top, "cols", [128, 40], F32)
        lamt = sbt(top, "lamt", [128, 8], F32)
        P = [top.enter_context(nc.psum_tensor(f"P{i}", [128, 1024], F32)) for i in range(4)]

        T.op('pool', [], ['ident'], lambda: nc.gpsimd.memset(ident[:], 1.0))
        T.op('pool', ['ident'], ['ident'], lambda: nc.gpsimd.affine_select(
            out=ident[:], in_=ident[:], pattern=[[-1, 128]], compare_op=ALU.is_equal, fill=0.0,
            base=0, channel_multiplier=1))
        T.op('dve', [], ['onesb'], lambda: nc.vector.memset(onesb[:], 1.0))
        T.op('dve', [], ['onesf'], lambda: nc.vector.memset(onesf[:], 1.0))
        T.op('dve', [], ['epsc'], lambda: nc.vector.memset(epsc[:], EPS))
        T.dma('sp', 'ld_cols', [], ['cols'], [(cols[:], colv[:, :])])

        with ExitStack() as es:
            cv = sbt(es, "cv", [128, 16], F32)
            cs = sbt(es, "cs", [128, 16], F32)
            rep = sbt(es, "rep", [128, 16, 128], BF16)
            wa = sbt(es, "wa", [128, 8, 2048], BF16)
            ba = sbt(es, "ba", [1, 9 * D], BF16)
            gbc = sbt(es, "gbc", [128, 6, D], F32)
            lq = sbt(es, "lq", [128, 256], F32)
            lqp = sbt(es, "lqp", [128, 128], F32)
            mt = [sbt(es, f"mt{i}", [128, 9, D], F32) for i in range(2)]
            tb = sbt(es, "tb", [128, 2, D], F32)
            T.dma('sp', 'misc', [], ['cv', 'gbc', 'lq'],
                  [(cv[:], cvec[:, :]), (lq[:], lam_qk[0:1, :].partition_broadcast(128))] +
                  [(gbc[:, i, :], norm_g[i:i + 1, :].partition_broadcast(128)) for i in range(6)])
            T.dma('pool', 'wts', [], ['ba'], [(ba[:], b_ada[:, :])])
            T.op('dve', ['lq'], ['lqp'], lambda: nc.vector.tensor_tensor(
                out=lqp[:].rearrange("p (a d) -> p a d", a=2),
                in0=lq[:].rearrange("p (a b d) -> p a b d", a=2, b=2)[:, :, 0, :],
                in1=lq[:].rearrange("p (a b d) -> p a b d", a=2, b=2)[:, :, 1, :], op=ALU.mult))
            T.op('dve', ['lqp'], ['lam0'], lambda: nc.vector.reduce_sum(
                out=lamt[:, 0:2], in_=lqp[:].rearrange("p (a d) -> p a d", a=2), axis=mybir.AxisListType.X))
            T.op('act', ['lam0'], ['lam1'], lambda: nc.scalar.activation(out=lamt[:, 2:4], in_=lamt[:, 0:2], func=AF.Exp))
            T.op('dve', ['lam1'], ['lam2'], lambda: nc.vector.tensor_tensor(
                out=lamt[:, 4:5], in0=lamt[:, 3:4], in1=lamt[:, 2:3], op=ALU.subtract))
            T.op('dve', ['lam2'], ['neglam'], lambda: nc.vector.tensor_scalar(
                out=lamt[:, 5:6], in0=lamt[:, 4:5], scalar1=-LAM_INIT, scalar2=None, op0=ALU.add))
            T.op('dve', ['cols'], ['sublc'], lambda: nc.vector.tensor_scalar(
                out=lamt[:, 6:7], in0=cols[:, 24:25], scalar1=1.0 - LAM_INIT, scalar2=None, op0=ALU.mult))
            T.op('act', ['cv'], ['cs'], lambda: nc.scalar.activation(out=cs[:], in_=cv[:], func=AF.Silu))
            for j in range(16):
                T.op('dve', ['cs', 'onesb'], ['rep'], lambda j=j: nc.vector.tensor_scalar(
                    out=rep[:, j, :], in0=onesb[:], scalar1=cs[:, j:j + 1], scalar2=None, op0=ALU.mult))
            wav = kc(w_ada)
            for k9 in range(9):
                sl = k9 % 2
                T.dma('pool', f'wa{sl}', [], [f'wa{sl}'],
                      [(wa[:, c, sl * 1024:(sl + 1) * 1024], wav[:, c, k9 * 1024:(k9 + 1) * 1024]) for c in range(8)])
                for which in range(2):
                    pp = P[which * 2 + 0] if True else None
                    pp = P[which]

                    def mm(which=which, sl=sl, k9=k9, pp=pp):
                        ins = None
                        for n in range(2):
                            for c in range(8):
                                nc.tensor.matmul(pp[:, n * 512:(n + 1) * 512], rep[:, which * 8 + c, :],
                                                 wa[:, c, sl * 1024 + n * 512: sl * 1024 + (n + 1) * 512],
                                                 start=(c == 0), stop=False)
                            ins = nc.tensor.matmul(pp[:, n * 512:(n + 1) * 512], onesb[0:1, :],
                                                   ba[0:1, k9 * 1024 + n * 512:k9 * 1024 + (n + 1) * 512],
                                                   start=False, stop=True)
                        return ins
                    T.op('pe', ['rep', f'wa{sl}', 'ba', 'onesb'], [f'P{which}'], mm)
                    T.op('act', [f'P{which}'], [f'mt{which}_{k9}'], lambda which=which, k9=k9, pp=pp:
                         nc.scalar.copy(out=mt[which][:, k9, :], in_=pp[:, :]))
            def mk_tab(idx, which, kind, k, gi, mult):
                m = mt[which]
                sl = idx % 2
                if kind == 'A':
                    T.op('dve', [f'mt{which}_{k + 1}', 'gbc'], [f'tb{sl}'], lambda: nc.vector.scalar_tensor_tensor(
                        out=tb[:, sl, :], in0=m[:, k + 1, :], scalar=1.0, in1=gbc[:, gi, :], op0=ALU.add, op1=ALU.mult))
                elif kind == 'B':
                    T.op('dve', [f'mt{which}_{k}'], [f'tb{sl}'], lambda: nc.vector.tensor_copy(tb[:, sl, :], m[:, k, :]))
                else:
                    T.op('dve', [f'mt{which}_{k + 2}', 'gbc'], [f'tb{sl}'], lambda: nc.vector.scalar_tensor_tensor(
                        out=tb[:, sl, :], in0=m[:, k + 2, :], scalar=mult, in1=gbc[:, gi, :], op0=ALU.mult, op1=ALU.mult))
                T.dma('sp', f'tbst{sl}', [f'tb{sl}'], [f'TAB{idx}'], [(TAB[idx], tb[:, sl, :])])
            specs = [(0, 'A', 0, 0, 0), (0, 'B', 0, 0, 0), (0, 'G', 0, 1, 0.5),
                     (0, 'A', 3, 2, 0), (0, 'B', 3, 0, 0), (0, 'G', 3, 3, 1.0),
                     (0, 'A', 6, 4, 0), (0, 'B', 6, 0, 0), (0, 'G', 6, 5, 0.5),
                     (1, 'A', 0, 0, 0), (1, 'B', 0, 0, 0), (1, 'G', 0, 1, 0.5),
                     (1, 'A', 3, 2, 0), (1, 'B', 3, 0, 0)]
            for idx, (which, kind, k, gi, mult) in enumerate(specs):
                mk_tab(idx, which, kind, k, gi, mult)
            skeys = ([f'TAB{i}' for i in range(14)] + ['wa0', 'wa1', 'ba', 'rep', 'gbc', 'lq', 'lqp', 'tb0', 'tb1', 'cs', 'cv', 'P0', 'P1']
                     + [f'mt{w}_{k}' for w in range(2) for k in range(9)])
            for e in ['pe', 'act', 'dve', 'pool', 'sp']:
                T.wait_all(e, skeys)

        def rms_prenorm(es_bufs, xs, xs_key, At, Bt, tabkeys, tag, s, junk_key='junk'):
            junk, st, tmp, hb = es_bufs
            c0 = (s % 2) * 4
            T.op('act', [xs_key], [junk_key, f'st{c0}'], lambda: nc.scalar.activation(
                out=junk[:], in_=xs, func=AF.Square, accum_out=st[:, c0:c0 + 1]))
            T.op('act', [f'st{c0}', 'epsc'], [f'st{c0 + 1}'], lambda: nc.scalar.activation(
                out=st[:, c0 + 1:c0 + 2], in_=st[:, c0:c0 + 1], func=AF.Sqrt, scale=1.0 / D, bias=epsc[:, 0:1]))
            T.op('dve', [f'st{c0 + 1}'], [f'st{c0 + 2}'], lambda: nc.vector.reciprocal(
                out=st[:, c0 + 2:c0 + 3], in_=st[:, c0 + 1:c0 + 2]))
            T.op('dve', [xs_key, f'st{c0 + 2}'] + tabkeys, ['tmp'], lambda: nc.vector.scalar_tensor_tensor(
                out=tmp[:], in0=xs, scalar=st[:, c0 + 2:c0 + 3], in1=At, op0=ALU.mult, op1=ALU.mult))
            T.op('dve', ['tmp'] + tabkeys, ['hb'], lambda: nc.vector.tensor_tensor(
                out=hb[:], in0=tmp[:], in1=Bt, op=ALU.add))

        def transp_to(hb, hT, s, nsub_key):
            half = s % 2
            pv = P[3][:].bitcast(BF16)
            def tr():
                ins = None
                for c in range(8):
                    ins = nc.tensor.transpose(pv[:, half * 1024 + c * 128: half * 1024 + (c + 1) * 128],
                                              hb[:, c * 128:(c + 1) * 128], ident[:])
                return ins
            T.op('pe', ['hb', 'ident'], [f'P3{half}'], tr)
            T.op('act', [f'P3{half}'], [nsub_key], lambda: nc.scalar.copy(
                out=hT[:, :, s * 128:(s + 1) * 128],
                in_=pv[:, half * 1024:(half + 1) * 1024].rearrange("p (c t) -> p c t", c=8)))

        def ffn_phase(wi, src, dst, tab_lat, tab_ctx, tiles):
            with ExitStack() as es:
                w13s = sbt(es, "w13s", [128, 8, 2 * DFF], BF16)
                w2s = sbt(es, "w2s", [128, NF, D], BF16)
                tabAB = sbt(es, "tabAB", [128, 2, D], F32)
                tabG = sbt(es, "tabG", [128, D], F32)
                xb = [sbt(es, f"xb{i}", [128, D], F32) for i in range(2)]
                hT2 = [sbt(es, f"hT{i}", [128, 8, 512], BF16) for i in range(2)]
                actT = sbt(es, "actT", [128, NF, 512], BF16)
                junk = sbt(es, "junk", [128, D], BF16)
                st = sbt(es, "st", [128, 16], F32)
                tmp = sbt(es, "tmp", [128, D], F32)
                hb = sbt(es, "hb", [128, D], BF16)
                sg = [sbt(es, f"sg{i}", [128, 512], F32) for i in range(2)]
                yb = sbt(es, "yb", [128, D], F32)
                w13v = kc(w13[wi])
                w2v = w2[wi].rearrange("(f p) n -> p f n", p=128)
                FG = [(0, 4), (4, 10), (10, 16), (16, NF)]
                fgrp = {}
                for g, (fs, fe) in enumerate(FG):
                    for f in range(fs, fe):
                        fgrp[f] = g
                    T.dma('pool', f'wtsg{g}', [], [f'w13s{g}'],
                          [(w13s[:, c, h * DFF + fs * 128: h * DFF + fe * 128], w13v[:, c, h * DFF + fs * 128: h * DFF + fe * 128])
                           for c in range(8) for h in range(2)])
                T.dma('pool', 'wts', [], ['w2s'], [(w2s[:, f, :], w2v[:, f, :]) for f in range(NF)])
                state = {'ab': None, 'g': None, 'xcnt': 0}

                def tabs_of(is_ctx):
                    return tab_ctx if is_ctx else tab_lat

                def prenorm_sub(i, s):
                    tok0, nsub, is_ctx = tiles[i]
                    tset = tabs_of(is_ctx)
                    if state['ab'] != tset:
                        T.dma('sp', 'tabs', [], ['tabAB'], [(tabAB[:, j, :], TAB[tset[j]]) for j in range(2)])
                        state['ab'] = tset
                    b = state['xcnt'] % 2; state['xcnt'] += 1
                    T.dma('sp', f'x{b}', [], [f'xb{b}'], [(xb[b][:], src[tok0 + s * 128: tok0 + (s + 1) * 128, :])])
                    rms_prenorm((junk, st, tmp, hb), xb[b][:], f'xb{b}', tabAB[:, 0, :], tabAB[:, 1, :], ['tabAB'], 'f', s)

                def transp_sub(i, s):
                    transp_to(hb, hT2[i % 2], s, f'hT{i % 2}')

                for s in range(tiles[0][1]):
                    prenorm_sub(0, s)
                    transp_sub(0, s)
                for i, (tok0, nsub, is_ctx) in enumerate(tiles):
                    NT = nsub * 128
                    hT = hT2[i % 2]; hk = f'hT{i % 2}'
                    nxt = tiles[i + 1] if i + 1 < len(tiles) else None
                    for f in range(NF):
                        if nxt is not None:
                            if f % 5 == 1 and f // 5 < nxt[1]:
                                prenorm_sub(i + 1, f // 5)
                            if f % 5 == 4 and f // 5 < nxt[1]:
                                transp_sub(i + 1, f // 5)
                        pp = P[f % 2]
                        def up(f=f, pp=pp):
                            ins = None
                            for half, col0 in ((0, f * 128), (1, DFF + f * 128)):
                                for c in range(8):
                                    ins = nc.tensor.matmul(pp[:, half * 512: half * 512 + NT], w13s[:, c, col0:col0 + 128],
                                                           hT[:, c, 0:NT], start=(c == 0), stop=(c == 7))
                            return ins
                        T.op('pe', [f'w13s{fgrp[f]}', hk], [f'P{f % 2}'], up)
                        T.op('act', [f'P{f % 2}'], [f'sg{f % 2}'], lambda f=f, pp=pp: nc.scalar.activation(
                            out=sg[f % 2][:, 0:NT], in_=pp[:, 0:NT], func=AF.Silu))
                        T.op('dve', [f'sg{f % 2}', f'P{f % 2}'], ['actT'], lambda f=f, pp=pp: nc.vector.tensor_tensor(
                            out=actT[:, f, 0:NT], in0=sg[f % 2][:, 0:NT], in1=pp[:, 512:512 + NT], op=ALU.mult))
                    tset = tabs_of(is_ctx)
                    if state['g'] != tset:
                        T.dma('sp', 'tabs', [], ['tabG'], [(tabG[:], TAB[tset[2]])])
                        state['g'] = tset
                    for s in range(nsub):
                        def down(s=s):
                            ins = None
                            for n in range(2):
                                for f in range(NF):
                                    ins = nc.tensor.matmul(P[2][:, n * 512:(n + 1) * 512], actT[:, f, s * 128:(s + 1) * 128],
                                                           w2s[:, f, n * 512:(n + 1) * 512], start=(f == 0), stop=(f == NF - 1))
                            return ins
                        T.op('pe', ['w2s', 'actT'], ['P2'], down)
                        b = state['xcnt'] % 2; state['xcnt'] += 1
                        T.dma('sp', f'x{b}', [], [f'xb{b}'], [(xb[b][:], src[tok0 + s * 128: tok0 + (s + 1) * 128, :])])
                        c0 = 8 + (s % 2) * 4
                        T.op('act', ['P2'], ['yb'], lambda: nc.scalar.copy(out=yb[:], in_=P[2][:, :]))
                        T.op('act', ['yb'], ['junk', f'st{c0}'], lambda c0=c0: nc.scalar.activation(
                            out=junk[:], in_=yb[:], func=AF.Square, accum_out=st[:, c0:c0 + 1]))
                        T.op('act', [f'st{c0}', 'epsc'], [f'st{c0 + 1}'], lambda c0=c0: nc.scalar.activation(
                            out=st[:, c0 + 1:c0 + 2], in_=st[:, c0:c0 + 1], func=AF.Sqrt, scale=1.0 / D, bias=epsc[:, 0:1]))
                        T.op('dve', [f'st{c0 + 1}'], [f'st{c0 + 2}'], lambda c0=c0: nc.vector.reciprocal(
                            out=st[:, c0 + 2:c0 + 3], in_=st[:, c0 + 1:c0 + 2]))
                        T.op('dve', ['yb', f'st{c0 + 2}', 'tabG'], ['yb'], lambda c0=c0: nc.vector.scalar_tensor_tensor(
                            out=yb[:], in0=yb[:], scalar=st[:, c0 + 2:c0 + 3], in1=tabG[:], op0=ALU.mult, op1=ALU.mult))
                        T.op('dve', ['yb', f'xb{b}'], [f'xb{b}'], lambda b=b: nc.vector.tensor_tensor(
                            out=xb[b][:], in0=yb[:], in1=xb[b][:], op=ALU.add))
                        T.dma('pool', f'xst{b}', [f'xb{b}'], [dst.tensor.name + str(tok0 + s * 128)],
                              [(dst[tok0 + s * 128: tok0 + (s + 1) * 128, :], xb[b][:])])
                phase_end(['w13s0', 'w13s1', 'w13s2', 'w13s3', 'w2s', 'tabAB', 'tabG', 'xb0', 'xb1', 'hT0', 'hT1', 'actT', 'junk', 'tmp', 'hb', 'sg0', 'sg1', 'yb']
                          + [f'st{i}' for i in range(16)])

        def phase_end(keys):
            allk = keys + ['P0', 'P1', 'P2', 'P30', 'P31']
            for e in ['pe', 'act', 'dve', 'pool', 'sp']:
                T.wait_all(e, allk)

        tilesA = [(i * 512, 4, False) for i in range(2 * NO // 512)] + [(2 * NO, 2, True)]
        ffn_phase(0, xc, X1, (0, 1, 2), (9, 10, 11), tilesA)
        for i_ in range(2):
            T.res[f'X1all{i_}'] = [(f'xst{i_}', T.dsem[f'xst{i_}'][1]), {}]

        def proj_phase():
            with ExitStack() as es:
                wq = sbt(es, "wq", [128, 8, 7 * D], BF16)
                qa = [sbt(es, f"qa{i}", [128, 512], BF16) for i in range(2)]
                perm = sbt(es, "perm", [128, 128], BF16)
                tabs = sbt(es, "tabs2", [128, 2, D], F32)
                xb = [sbt(es, f"xb{i}", [128, D], F32) for i in range(2)]
                hT2 = [sbt(es, f"hT{i}", [128, 8, 512], BF16) for i in range(2)]
                st = sbt(es, "st", [128, 8], F32)
                tmp = sbt(es, "tmp", [128, D], F32)
                hb = sbt(es, "hb", [128, D], BF16)
                cst = sbt(es, "cst", [128, 2, 512], F32)
                t1 = [sbt(es, f"t1{i}", [128, 512], F32) for i in range(2)]
                t2 = [sbt(es, f"t2{i}", [128, 512], F32) for i in range(2)]
                kst = sbt(es, "kst", [128, 8, 512], BF16)
                vst = sbt(es, "vst", [128, D], BF16)
                sgt1 = sbt(es, "sgt", [128, 512], F32)
                sgt = [sgt1, sgt1]
                win_v = kc(w_in)
                iv = ident[:].rearrange("p (b h j) -> p b h j", b=4, h=2)
                pv_ = perm[:].rearrange("p (b h j) -> p b h j", b=4, h=2)
                def mkperm():
                    nc.vector.tensor_copy(pv_[:, :, 0, :], iv[:, :, 1, :])
                    return nc.vector.tensor_copy(pv_[:, :, 1, :], iv[:, :, 0, :])
                T.op('dve', ['ident'], ['perm'], mkperm)
                for (wt_, wv_, nm, blk) in ((wq, win_v, 'wq', 1), (wq, win_v, 'wq', 2), (wq, win_v, 'wq', 0),
                                            (wq, win_v, 'wq', 3), (wq, win_v, 'wq', 4), (wq, win_v, 'wq', 5),
                                            (wq, win_v, 'wq', 6)):
                    T.dma('pool', f'wts_{nm}{blk}', [], [f'{nm}{blk}'],
                          [(wt_[:, c, blk * D:(blk + 1) * D], wv_[:, c, blk * D:(blk + 1) * D]) for c in range(8)])
                state = {'tab': None, 'xcnt': 0}
                pcnt = [0]
                kq_cnt = [0]
                tiles = [(i * 512, 4, False, i < NTO) for i in range(2 * NTO)] + [(2 * NO, 2, True, False)]

                def prenorm_sub(ti, s):
                    tok0, nsub, is_ctx, own = tiles[ti]
                    tset = (12, 13) if is_ctx else (3, 4)
                    if state['tab'] != tset:
                        T.dma('sp', 'tabs', [], ['tabs'], [(tabs[:, i, :], TAB[tset[i]]) for i in range(2)])
                        state['tab'] = tset
                    b = state['xcnt'] % 2; state['xcnt'] += 1
                    T.dma('sp', f'x{b}', ['X1all0', 'X1all1'], [f'xb{b}'], [(xb[b][:], X1[tok0 + s * 128: tok0 + (s + 1) * 128, :])])
                    rms_prenorm((hb, st, tmp, hb), xb[b][:], f'xb{b}', tabs[:, 0, :], tabs[:, 1, :], ['tabs'], 'p', s, junk_key='hb')

                def transp_sub(ti, s):
                    transp_to(hb, hT2[ti % 2], s, f'hT{ti % 2}')

                for s in range(tiles[0][1]):
                    prenorm_sub(0, s)
                    transp_sub(0, s)
                tick = [0]
                for ti, (tok0, nsub, is_ctx, own) in enumerate(tiles):
                    NT = nsub * 128
                    hT = hT2[ti % 2]; hk = f'hT{ti % 2}'
                    nxt = tiles[ti + 1] if ti + 1 < len(tiles) else None
                    tick[0] = 0
                    T.dma('sp', 'cst', [], ['cst'], [(cst[:, 0, 0:NT], cosT[:, tok0:tok0 + NT]), (cst[:, 1, 0:NT], sinT[:, tok0:tok0 + NT])])

                    def pe_tick():
                        k = tick[0]; tick[0] += 1
                        if nxt is None:
                            return
                        if k % 3 == 0 and k // 3 < nxt[1]:
                            prenorm_sub(ti + 1, k // 3)
                        if k % 3 == 2 and k // 3 < nxt[1]:
                            transp_sub(ti + 1, k // 3)

                    def pair_mm(wa_, ca, wb_, cb):
                        pe_tick()
                        i = pcnt[0] % 2; pcnt[0] += 1
                        pp = P[i]
                        def mm():
                            ins = None
                            for half, (w_, c0) in enumerate(((wa_, ca), (wb_, cb))):
                                for c in range(8):
                                    ins = nc.tensor.matmul(pp[:, half * 512: half * 512 + NT], w_[:, c, c0:c0 + 128],
                                                           hT[:, c, 0:NT], start=(c == 0), stop=(c == 7))
                            return ins
                        ka = 'wq' + str(ca // D)
                        kb = 'wq' + str(cb // D)
                        T.op('pe', [ka, kb, hk], [f'P{i}a', f'P{i}b'], mm)
                        return i

                    def rope_proj(col0, dstT, dcol0):
                        pend = []

                        def finish(p, i, j):
                            def rmm():
                                return nc.tensor.matmul(P[i][:, 512:512 + NT], perm[:], qa[j][:, 0:NT], start=True, stop=True)
                            T.op('pe', [f'qa{j}', 'perm'], [f'P{i}b'], rmm)
                            T.op('dve', [f'P{i}a', f'qa{j}', 'cst'], [f't1{j}'], lambda: nc.vector.tensor_tensor(
                                out=t1[j][:, 0:NT], in0=P[i][:, 0:NT], in1=cst[:, 0, 0:NT], op=ALU.mult))
                            T.op('dve', [f'P{i}b', 'cst'], [f't2{j}'], lambda: nc.vector.tensor_tensor(
                                out=t2[j][:, 0:NT], in0=P[i][:, 512:512 + NT], in1=cst[:, 1, 0:NT], op=ALU.mult))
                            T.op('pool', [f't1{j}', f't2{j}'], ['kst'], lambda: nc.gpsimd.tensor_tensor(
                                out=kst[:, p, 0:NT], in0=t1[j][:, 0:NT], in1=t2[j][:, 0:NT], op=ALU.add))
                        for p in range(8):
                            pe_tick()
                            i = pcnt[0] % 2; pcnt[0] += 1
                            j = kq_cnt[0] % 2; kq_cnt[0] += 1
                            def mm(i=i, p=p):
                                ins = None
                                for c in range(8):
                                    ins = nc.tensor.matmul(P[i][:, 0:NT], wq[:, c, col0 + p * 128: col0 + (p + 1) * 128],
                                                           hT[:, c, 0:NT], start=(c == 0), stop=(c == 7))
                                return ins
                            T.op('pe', ['wq' + str(col0 // D), hk], [f'P{i}a'], mm)
                            T.op('act', [f'P{i}a'], [f'qa{j}'], lambda i=i, j=j: nc.scalar.copy(out=qa[j][:, 0:NT], in_=P[i][:, 0:NT]))
                            if pend:
                                finish(*pend.pop())
                            pend.append((p, i, j))
                        finish(*pend.pop())
                        T.dma('pool', 'kst', ['kst'], [dstT.tensor.name + str(dcol0)],
                              [(dstT.rearrange("g p t -> p g t")[:, :, dcol0:dcol0 + NT], kst[:, :, 0:NT])])

                    def glu_proj(ucol0, mask_col):
                        for c8 in range(8):
                            i = pair_mm(wq, 3 * D + c8 * 128, wq, 4 * D + c8 * 128)
                            j = 0
                            T.op('act', [f'P{i}b'], [f'sgt{j}'], lambda i=i, j=j: nc.scalar.activation(
                                out=sgt[j][:, 0:NT], in_=P[i][:, 512:512 + NT], func=AF.Sigmoid))
                            if mask_col is None:
                                T.op('dve', [f'sgt{j}', f'P{i}a'], ['kst'], lambda i=i, j=j, c8=c8: nc.vector.tensor_tensor(
                                    out=kst[:, c8, 0:NT], in0=sgt[j][:, 0:NT], in1=P[i][:, 0:NT], op=ALU.mult))
                            else:
                                T.op('dve', [f'sgt{j}', f'P{i}a', 'cols'], ['kst'], lambda i=i, j=j, c8=c8: nc.vector.scalar_tensor_tensor(
                                    out=kst[:, c8, 0:NT], in0=sgt[j][:, 0:NT], scalar=cols[:, mask_col:mask_col + 1],
                                    in1=P[i][:, 0:NT], op0=ALU.mult, op1=ALU.mult))
                        T.dma('pool', 'kst', ['kst'], ['U' + str(ucol0)], [(U[:, :, ucol0:ucol0 + NT], kst[:, :, 0:NT])])

                    def gate_proj(col0, dstT):
                        for c8 in range(0, 8, 2):
                            i = pair_mm(wq, col0 + c8 * 128, wq, col0 + (c8 + 1) * 128)
                            T.op('act', [f'P{i}a', f'P{i}b'], ['kst'], lambda i=i, c8=c8: nc.scalar.activation(
                                out=kst[:, c8:c8 + 2, 0:NT], in_=P[i][:].rearrange("p (a t) -> p a t", a=2)[:, :, 0:NT], func=AF.Sigmoid))
                        T.dma('pool', 'kst', ['kst'], [dstT.tensor.name + str(tok0)], [(dstT[:, :, tok0:tok0 + NT], kst[:, :, 0:NT])])

                    rope_proj(D, KT, tok0)
                    for s in range(nsub):
                        def vmm(s=s):
                            ins = None
                            for n in range(2):
                                for c in range(8):
                                    ins = nc.tensor.matmul(P[2][:, n * 512:(n + 1) * 512], hT[:, c, s * 128:(s + 1) * 128],
                                                           wq[:, c, 2 * D + n * 512: 2 * D + (n + 1) * 512], start=(c == 0), stop=(c == 7))
                            return ins
                        pe_tick()
                        T.op('pe', ['wq2', hk], ['P2'], vmm)
                        T.op('act', ['P2'], ['vst'], lambda: nc.scalar.copy(out=vst[:], in_=P[2][:, :]))
                        T.dma('pool', 'vst', ['vst'], ['V' + str(tok0 + s * 128)], [(V[tok0 + s * 128: tok0 + (s + 1) * 128, :], vst[:])])
                    if own:
                        rope_proj(0, QT, tok0)
                        glu_proj(512 + tok0, None)
                        gate_proj(5 * D, SGA)
                        gate_proj(6 * D, SGB)
                    elif ti == NTO:
                        glu_proj(512 + NO, 26)
                    if ti == 2 * NTO - 1:
                        glu_proj(0, 25)
                phase_end([f'wq{i}' for i in range(7)] + ['qa0', 'qa1', 'perm', 'P0a', 'P0b', 'P1a', 'P1b', 'tabs', 'xb0', 'xb1', 'hT0', 'hT1', 'tmp', 'hb', 'cst', 't10', 't11', 't20', 't21',
                           'kst', 'vst', 'sgt0', 'sgt1'] + [f'st{i}' for i in range(8)])
        proj_phase()
        T.res['A2all'] = [None, {}]
        for nm in ['kst', 'vst']:
            pass

        def store_tok(name):
            return [(name, T.dsem[name][1]), {}]

        def attn_phase():
            T.res['KVQ'] = store_tok('kst'); T.res['Vall'] = store_tok('vst')
            with ExitStack() as es:
                ktb = [sbt(es, f"ktb{i}", [128, NALL], BF16) for i in range(2)]
                vb = [sbt(es, f"vb{i}", [128, NKT, 128], BF16) for i in range(2)]
                qtb = [sbt(es, f"qtb{i}", [128, NO], BF16) for i in range(2)]
                NPT = 6
                pT = [sbt(es, f"pT{i}", [128, 1024], BF16) for i in range(NPT)]
                pa = [sbt(es, f"pa{i}", [128, 1024], BF16) for i in range(2)]
                pq = [sbt(es, f"pq{i}", [128, 1024], BF16) for i in range(2)]
                po = [sbt(es, f"po{i}", [128, 1024], BF16) for i in range(2)]
                rr = [sbt(es, f"rr{i}", [128, 512], F32) for i in range(2)]
                oo = [sbt(es, f"oo{i}", [128, 512], F32) for i in range(2)]
                dd = sbt(es, "dd", [128, 512], F32)
                sq = sbt(es, "sq", [128, 512], BF16)
                sd = sbt(es, "sd", [128, 512], F32)
                ost = [sbt(es, f"ost{i}", [128, 512], BF16) for i in range(2)]
                Vv = V.rearrange("(kt p) (h e) -> h p kt e", p=128, e=128)
                PO = [P[2][:, 0:512], P[2][:, 512:1024]]
                PL = [P[3][:, 0:512], P[3][:, 512:1024]]
                NO8 = NKT // 8
                NTAILP = (NKT - 8 * NO8) // 2
                NLM = NO8 + NTAILP
                st_ = {'gcnt': 0, 'ocnt': 0}

                def load_pair(pr):
                    b = pr % 2
                    T.dma('sp', f'kv{b}', ['KVQ', 'Vall'], [f'ktb{b}', f'vb{b}', f'qtb{b}'],
                          [(ktb[b][:], KT[pr]), (qtb[b][:], QT[pr])] +
                          [(vb[b][:, k0:min(k0 + 8, NKT), :], Vv[pr][:, k0:min(k0 + 8, NKT), :]) for k0 in range(0, NKT, 8)])

                def epi_A():
                    for sub in range(2):
                        T.op('act', ['PL'], [f'rr{sub}'], lambda sub=sub: nc.scalar.activation(out=rr[sub][:], in_=PL[sub], func=AF.Ln))
                    for sub in range(2):
                        T.op('dve', ['PO'], [f'oo{sub}'], lambda sub=sub: nc.vector.tensor_copy(oo[sub][:], PO[sub]))

                def epi_B():
                    for sub in range(2):
                        T.op('act', [f'rr{sub}'], [f'rr{sub}'], lambda sub=sub: nc.scalar.activation(
                            out=rr[sub][:], in_=rr[sub][:], func=AF.Exp, scale=-1.0))
                    for sub in range(2):
                        T.op('dve', [f'oo{sub}', f'rr{sub}'], [f'oo{sub}'], lambda sub=sub: nc.vector.tensor_tensor(
                            out=oo[sub][:], in0=oo[sub][:], in1=rr[sub][:], op=ALU.mult))
                    T.op('dve', ['oo0', 'oo1', 'neglam'], ['dd'], lambda: nc.vector.scalar_tensor_tensor(
                        out=dd[:], in0=oo[1][:], scalar=lamt[:, 5:6], in1=oo[0][:], op0=ALU.mult, op1=ALU.add))
                    T.op('dve', ['dd'], ['sq'], lambda: nc.vector.tensor_tensor(out=sq[:], in0=dd[:], in1=dd[:], op=ALU.mult))

                def epi_C():
                    T.op('pe', ['sq', 'onesb'], ['PL'], lambda: nc.tensor.matmul(PL[0], onesb[:], sq[:], start=True, stop=True))

                def epi_D(pr, qt):
                    T.op('act', ['PL', 'epsc'], ['sd'], lambda: nc.scalar.activation(
                        out=sd[:], in_=PL[0], func=AF.Ln, scale=1.0 / 128, bias=epsc[:, 0:1]))
                    T.op('act', ['sd'], ['sd'], lambda: nc.scalar.activation(out=sd[:], in_=sd[:], func=AF.Exp, scale=-0.5))
                    ob = st_['ocnt'] % 2; st_['ocnt'] += 1
                    T.op('dve', ['sd', 'dd', 'sublc'], [f'ost{ob}'], lambda: nc.vector.scalar_tensor_tensor(
                        out=ost[ob][:], in0=dd[:], scalar=lamt[:, 6:7], in1=sd[:], op0=ALU.mult, op1=ALU.mult))
                    T.dma('sp', f'ost{ob}', [f'ost{ob}'], [f'OT{pr}_{qt}'], [(OT[:, pr, qt * 512:(qt + 1) * 512], ost[ob][:])])

                prev = None
                load_pair(0)
                for pr in range(8):
                    b = pr % 2
                    if pr + 1 < 8:
                        load_pair(pr + 1)
                    for qt in range(NTO):
                        g0 = st_['gcnt']

                        def qk(kt):
                            sb_ = (g0 + kt) % 2
                            def f():
                                ins = None
                                for sub in range(2):
                                    rs = slice(sub * 64, (sub + 1) * 64)
                                    ins = nc.tensor.matmul(P[sb_][:, sub * 512:(sub + 1) * 512], ktb[b][rs, kt * 128:(kt + 1) * 128],
                                                           qtb[b][rs, qt * 512:(qt + 1) * 512], start=True, stop=True)
                                return ins
                            T.op('pe', [f'ktb{b}', f'qtb{b}'], [f'P{sb_}'], f)

                        def ex(kt):
                            sb_ = (g0 + kt) % 2; pb = (g0 + kt) % NPT
                            T.op('act', [f'P{sb_}'], [f'pT{pb}'], lambda: nc.scalar.activation(
                                out=pT[pb][:], in_=P[sb_][:, :], func=AF.Exp, scale=0.125))

                        def pv(kt):
                            pb = (g0 + kt) % NPT
                            def f():
                                ins = None
                                for sub in range(2):
                                    ins = nc.tensor.matmul(PO[sub], vb[b][:, kt, :], pT[pb][:, sub * 512:(sub + 1) * 512],
                                                           start=(kt == 0), stop=(kt == NKT - 1))
                                return ins
                            T.op('pe', [f'vb{b}', f'pT{pb}'], ['PO'], f)

                        def padd(j):
                            p0 = (g0 + 2 * j) % NPT; p1 = (g0 + 2 * j + 1) % NPT; q = j % 2
                            T.op('dve', [f'pT{p0}', f'pT{p1}'], [f'pa{q}'], lambda: nc.vector.tensor_tensor(
                                out=pa[q][:], in0=pT[p0][:], in1=pT[p1][:], op=ALU.add))

                        def qadd(m):
                            T.op('dve', ['pa0', 'pa1'], [f'pq{m % 2}'], lambda: nc.vector.tensor_tensor(
                                out=pq[m % 2][:], in0=pa[0][:], in1=pa[1][:], op=ALU.add))

                        def lmm(src, key, idx):
                            def f():
                                ins = None
                                for sub in range(2):
                                    ins = nc.tensor.matmul(PL[sub], onesb[:], src[:, sub * 512:(sub + 1) * 512],
                                                           start=(idx == 0), stop=(idx == NLM - 1))
                                return ins
                            T.op('pe', [key, 'onesb'], ['PL'], f)
                        qk(0)
                        qk(1)
                        for kt in range(NKT):
                            ex(kt)
                            if kt + 2 < NKT:
                                qk(kt + 2)
                            pv(kt)
                            if prev is not None:
                                if kt == 0:
                                    epi_B()
                                elif kt == 2:
                                    epi_C()
                                elif kt == 3:
                                    epi_D(*prev)
                            if kt % 2 == 1 and kt < 8 * NO8:
                                padd(kt // 2)
                            if kt % 4 == 3 and kt < 8 * NO8:
                                qadd(kt // 4)
                            if kt % 8 == 7 and kt < 8 * NO8:
                                o = kt // 8
                                T.op('dve', ['pq0', 'pq1'], [f'po{o % 2}'], lambda o=o: nc.vector.tensor_tensor(
                                    out=po[o % 2][:], in0=pq[0][:], in1=pq[1][:], op=ALU.add))
                            if kt % 8 == 1 and kt >= 9 and kt - 2 < 8 * NO8:
                                o = kt // 8 - 1
                                lmm(po[o % 2], f'po{o % 2}', o)
                        if NO8 >= 1 and (8 * NO8 - 1 + 2) >= NKT:
                            o = NO8 - 1
                            lmm(po[o % 2], f'po{o % 2}', o)
                        for tp in range(NTAILP):
                            j = 4 * NO8 + tp
                            p0 = (g0 + 2 * j) % NPT; p1 = (g0 + 2 * j + 1) % NPT
                            T.op('dve', [f'pT{p0}', f'pT{p1}'], [f'pa{tp % 2}'], lambda p0=p0, p1=p1, tp=tp: nc.vector.tensor_tensor(
                                out=pa[tp % 2][:], in0=pT[p0][:], in1=pT[p1][:], op=ALU.add))
                            lmm(pa[tp % 2], f'pa{tp % 2}', NO8 + tp)
                        st_['gcnt'] += NKT
                        epi_A()
                        prev = (pr, qt)
                epi_B(); epi_C(); epi_D(*prev)
                phase_end(['ktb0', 'ktb1', 'vb0', 'vb1', 'qtb0', 'qtb1'] + [f'pT{i}' for i in range(NPT)] + ['rr0', 'rr1', 'oo0', 'oo1',
                           'dd', 'sq', 'sd', 'ost0', 'ost1', 'PO', 'PL', 'pa0', 'pa1', 'pq0', 'pq1', 'po0', 'po1'])

        dg_es = ExitStack()
        cw = sbt(dg_es, "cw", [128, 8, CONV_K], F32)
        dg = sbt(dg_es, "dg", [128, 8 * CONV_K, 128], BF16)
        T.dma('sp', 'ld_cw', [], ['cw'], [(cw[:], conv_w[:, :, :])])
        for c8 in range(8):
            def bld(c8=c8):
                ins = None
                for j in range(CONV_K):
                    ins = nc.gpsimd.tensor_scalar(out=dg[:, c8 * CONV_K + j, :], in0=ident[:], scalar1=cw[:, c8, j:j + 1],
                                                  scalar2=None, op0=ALU.mult)
                return ins
            T.op('pool', ['cw', 'ident'], [f'dg{c8}'], bld)
        attn_phase()

        def merge_phase():
            TW = 256
            NTL = NO // TW
            for i_ in range(2):
                T.res[f'OTall{i_}'] = store_tok(f'ost{i_}')
            with ExitStack() as es:
                wap = sbt(es, "wap", [128, 8, D], BF16)
                wcp = sbt(es, "wcp", [128, 8, D], BF16)
                wo = sbt(es, "wo", [128, 8, D], BF16)
                tabs = sbt(es, "tabs3", [128, D], F32)
                ub = [sbt(es, f"ub{i}", [128, 8, TW + 32], BF16) for i in range(2)]
                sga = [sbt(es, f"sga{i}", [128, 8, TW], BF16) for i in range(2)]
                ot = [sbt(es, f"ot{i}", [128, 8, TW], BF16) for i in range(2)]
                sgb = sbt(es, "sgb", [128, 8, TW], BF16)
                vv = [sbt(es, f"vv{i}", [128, 8, TW], F32) for i in range(2)]
                sqv = [sbt(es, f"sqv{i}", [128, TW], F32) for i in range(2)]
                z2 = [sbt(es, f"z2{i}", [128, 8, TW], BF16) for i in range(2)]
                mean = sbt(es, "mean", [128, TW], F32)
                msq = sbt(es, "msq", [128, TW], F32)
                rstd = sbt(es, "rstd", [128, TW], F32)
                mr = sbt(es, "mr", [128, TW], F32)
                tt = [sbt(es, f"tt{i}", [128, TW], F32) for i in range(2)]
                cT = sbt(es, "cT", [128, 8, TW], BF16)
                z1 = [sbt(es, f"z1{i}", [128, TW], F32) for i in range(2)]
                zT = sbt(es, "zT", [128, 8, TW], BF16)
                xb = [sbt(es, f"xb{i}", [128, D], F32) for i in range(2)]
                junk = sbt(es, "junk", [128, D], BF16)
                st = sbt(es, "st", [128, 8], F32)
                yb = sbt(es, "yb", [128, D], F32)
                for (wt, wd, nm) in ((wap, w_ap, 'wap'), (wcp, w_cp, 'wcp'), (wo, w_o, 'wo')):
                    T.dma('pool', 'wts_' + nm, [], [nm], [(wt[:, c, :], kc(wd)[:, c, :]) for c in range(8)])
                T.dma('sp', 'tabs', [], ['tabs'], [(tabs[:], TAB[5])])
                stt_ = {'xcnt': 0}

                def load_in(i):
                    t0 = i * TW; q = i % 2
                    T.dma('sp', f'c1in{q}', ['KVQ', 'OTall0', 'OTall1'], [f'ub{q}', f'sga{q}', f'ot{q}'],
                          [(ub[q][:, :, 0:TW + 30], U[:, :, 512 + t0 - 15: 512 + t0 + TW + 15]),
                           (sga[q][:], SGA[:, :, t0:t0 + TW]), (ot[q][:], OT[:, :, t0:t0 + TW])])

                def ln_head(i):
                    T.op('dve', ['P2'], ['mean'], lambda: nc.vector.tensor_scalar(
                        out=mean[:], in0=P[2][:, 0:TW], scalar1=1.0 / D, scalar2=None, op0=ALU.mult))
                    T.op('dve', ['mean'], ['msq'], lambda: nc.vector.tensor_tensor(out=msq[:], in0=mean[:], in1=mean[:], op=ALU.mult))
                    T.op('dve', ['P2', 'msq'], ['rstd'], lambda: nc.vector.scalar_tensor_tensor(
                        out=rstd[:], in0=P[2][:, 512:512 + TW], scalar=1.0 / D, in1=msq[:], op0=ALU.mult, op1=ALU.subtract))
                    T.op('act', ['rstd', 'epsc'], ['rstd'], lambda: nc.scalar.activation(
                        out=rstd[:], in_=rstd[:], func=AF.Ln, scale=1.0, bias=epsc[:, 0:1]))
                    T.op('act', ['rstd'], ['rstd'], lambda: nc.scalar.activation(out=rstd[:], in_=rstd[:], func=AF.Exp, scale=-0.5))
                    T.op('dve', ['mean', 'rstd'], ['mr'], lambda: nc.vector.tensor_tensor(out=mr[:], in0=mean[:], in1=rstd[:], op=ALU.mult))

                def ln_chunk(i, c8):
                    q = i % 2; j = c8 % 2
                    T.op('dve', [f'vv{q}_{c8}', 'rstd'], [f'tt{j}'], lambda: nc.vector.tensor_tensor(
                        out=tt[j][:], in0=vv[q][:, c8, :], in1=rstd[:], op=ALU.mult))
                    T.op('dve', [f'tt{j}', 'mr'], [f'tt{j}'], lambda: nc.vector.tensor_tensor(
                        out=tt[j][:], in0=tt[j][:], in1=mr[:], op=ALU.subtract))
                    T.op('act', [f'tt{j}', 'cols'], ['cT'], lambda: nc.scalar.activation(
                        out=cT[:, c8, :], in_=tt[j][:], func=AF.Silu, scale=cols[:, 8 + c8:9 + c8], bias=cols[:, 16 + c8:17 + c8]))

                def conv_chunk(i, c8):
                    q = i % 2; j = c8 % 2
                    def cmm():
                        ins = None
                        for k in range(CONV_K):
                            ins = nc.tensor.matmul(P[j][:, 0:TW], dg[:, c8 * CONV_K + k, :], ub[q][:, c8, k:k + TW],
                                                   start=(k == 0), stop=(k == CONV_K - 1))
                        return ins
                    T.op('pe', [f'dg{c8}', f'ub{q}'], [f'P{j}a'], cmm)
                    T.op('act', [f'P{j}a', 'cols'], [f'vv{q}_{c8}'], lambda: nc.scalar.activation(
                        out=vv[q][:, c8, :], in_=P[j][:, 0:TW], func=AF.Identity, bias=cols[:, c8:c8 + 1], scale=1.0))
                    T.op('dve', [f'vv{q}_{c8}'], [f'sqv{j}'], lambda: nc.vector.tensor_tensor(
                        out=sqv[j][:], in0=vv[q][:, c8, :], in1=vv[q][:, c8, :], op=ALU.mult))
                    def amm():
                        ins = None
                        for c in range(8):
                            ins = nc.tensor.matmul(P[j][:, 512:512 + TW], wap[:, c, c8 * 128:(c8 + 1) * 128], ot[q][:, c, :],
                                                   start=(c == 0), stop=(c == 7))
                        return ins
                    T.op('pe', ['wap', f'ot{q}'], [f'P{j}b'], amm)
                    T.op('dve', [f'P{j}b', f'sga{q}', f'vv{q}_{c8}'], [f'z2{q}'], lambda: nc.vector.tensor_tensor(
                        out=z2[q][:, c8, :], in0=P[j][:, 512:512 + TW], in1=sga[q][:, c8, :], op=ALU.mult))
                    def smm():
                        nc.tensor.matmul(P[2][:, 0:TW], onesf[:], vv[q][:, c8, :], start=(c8 == 0), stop=(c8 == 7))
                        return nc.tensor.matmul(P[2][:, 512:512 + TW], onesf[:], sqv[j][:], start=(c8 == 0), stop=(c8 == 7))
                    T.op('pe', [f'vv{q}_{c8}', f'sqv{j}', 'onesf'], ['P2'], smm)

                def out_stage(i):
                    t0 = i * TW; q = i % 2
                    T.dma('sp', 'sgb', ['KVQ'], ['sgb'], [(sgb[:], SGB[:, :, t0:t0 + TW])])
                    for d8 in range(8):
                        j = d8 % 2
                        def bmm(d8=d8, j=j):
                            ins = None
                            for c in range(8):
                                ins = nc.tensor.matmul(P[j][:, 0:TW], wcp[:, c, d8 * 128:(d8 + 1) * 128], cT[:, c, :], start=(c == 0), stop=(c == 7))
                            return ins
                        T.op('pe', ['wcp', 'cT'], [f'P{j}a'], bmm)
                        T.op('dve', [f'P{j}a', 'sgb'], [f'z1{j}'], lambda d8=d8, j=j: nc.vector.tensor_tensor(
                            out=z1[j][:], in0=P[j][:, 0:TW], in1=sgb[:, d8, :], op=ALU.mult))
                        T.op('dve', [f'z1{j}', f'z2{q}'], ['zT'], lambda d8=d8, j=j: nc.vector.tensor_tensor(
                            out=zT[:, d8, :], in0=z1[j][:], in1=z2[q][:, d8, :], op=ALU.add))
                    for s in range(TW // 128):
                        def omm(s=s):
                            ins = None
                            for n in range(2):
                                for c in range(8):
                                    ins = nc.tensor.matmul(P[3][:, n * 512:(n + 1) * 512], zT[:, c, s * 128:(s + 1) * 128],
                                                           wo[:, c, n * 512:(n + 1) * 512], start=(c == 0), stop=(c == 7))
                            return ins
                        T.op('pe', ['wo', 'zT'], ['P3'], omm)
                        b = stt_['xcnt'] % 2; stt_['xcnt'] += 1
                        T.dma('sp', f'x{b}', ['X1all0', 'X1all1'], [f'xb{b}'], [(xb[b][:], X1[t0 + s * 128: t0 + (s + 1) * 128, :])])
                        c0 = (s % 2) * 4
                        T.op('act', ['P3'], ['yb'], lambda: nc.scalar.copy(out=yb[:], in_=P[3][:, :]))
                        T.op('act', ['yb'], ['junk', f'st{c0}'], lambda c0=c0: nc.scalar.activation(
                            out=junk[:], in_=yb[:], func=AF.Square, accum_out=st[:, c0:c0 + 1]))
                        T.op('act', [f'st{c0}', 'epsc'], [f'st{c0 + 1}'], lambda c0=c0: nc.scalar.activation(
                            out=st[:, c0 + 1:c0 + 2], in_=st[:, c0:c0 + 1], func=AF.Ln, scale=1.0 / D, bias=epsc[:, 0:1]))
                        T.op('act', [f'st{c0 + 1}'], [f'st{c0 + 2}'], lambda c0=c0: nc.scalar.activation(
                            out=st[:, c0 + 2:c0 + 3], in_=st[:, c0 + 1:c0 + 2], func=AF.Exp, scale=-0.5))
                        T.op('dve', ['yb', f'st{c0 + 2}', 'tabs'], ['yb'], lambda c0=c0: nc.vector.scalar_tensor_tensor(
                            out=yb[:], in0=yb[:], scalar=st[:, c0 + 2:c0 + 3], in1=tabs[:], op0=ALU.mult, op1=ALU.mult))
                        T.op('dve', ['yb', f'xb{b}'], [f'xb{b}'], lambda b=b: nc.vector.tensor_tensor(
                            out=xb[b][:], in0=yb[:], in1=xb[b][:], op=ALU.add))
                        T.dma('pool', f'xst{b}', [f'xb{b}'], ['X2' + str(t0 + s * 128)], [(X2[t0 + s * 128: t0 + (s + 1) * 128, :], xb[b][:])])

                load_in(0)
                if NTL > 1:
                    load_in(1)
                for c8 in range(8):
                    conv_chunk(0, c8)
                ln_head(0)
                for i in range(NTL):
                    if i + 1 < NTL:
                        for c8 in range(8):
                            ln_chunk(i, c8)
                            conv_chunk(i + 1, c8)
                        ln_head(i + 1)
                        if i + 2 < NTL:
                            load_in(i + 2)
                    else:
                        for c8 in range(8):
                            ln_chunk(i, c8)
                    out_stage(i)
                phase_end(['wap', 'wcp', 'wo', 'cw', 'tabs', 'ub0', 'ub1', 'sga0', 'sga1', 'ot0', 'ot1', 'sgb', 'sqv0', 'sqv1', 'mean', 'msq',
                           'rstd', 'mr', 'tt0', 'tt1', 'cT', 'z10', 'z11', 'z20', 'z21', 'zT', 'xb0', 'xb1', 'junk', 'yb',
                           'P0a', 'P0b', 'P1a', 'P1b', 'P3']
                          + [f'vv{q}_{i}' for i in range(8) for q in range(2)] + [f'st{i}' for i in range(8)])
        merge_phase()
        for e in ['pe', 'act', 'dve', 'pool', 'sp']:
            T.wait_all(e, [f'dg{c8}' for c8 in range(8)] + ['cw'])
        dg_es.close()
        for i_ in range(2):
            T.res[f'X2all{i_}'] = store_tok(f'xst{i_}')
        for e in ['sp']:
            T.wait_all(e, ['X2all0', 'X2all1'])
        ffn_phase(1, X2, out, (6, 7, 8), (6, 7, 8), [(i * 512, 4, False) for i in range(NTO)])
        for i_ in range(2):
            nc.sync.wait_ge(T.dsem[f'xst{i_}'][0], T.dsem[f'xst{i_}'][1])
            nc.gpsimd.wait_ge(T.dsem[f'xst{i_}'][0], T.dsem[f'xst{i_}'][1])

        if dbg:
            for nm in ['xst0', 'xst1', 'kst', 'vst', 'ost0', 'ost1']:
                nc.sync.wait_ge(T.dsem[nm][0], T.dsem[nm][1])
    return nc


def _host_prep(inputs, NO, n_cores, batch_of, half_of):
    x = np.asarray(inputs['x'], np.float32)
    ctx = np.asarray(inputs['ctx'], np.float32)
    c = np.asarray(inputs['c'], np.float32)
    c_ctx = np.asarray(inputs['c_ctx'], np.float32)
    NTOK = 2 * NO
    NALL = NTOK + CTX
    w_in = np.ascontiguousarray(np.asarray(inputs['w_in'], np.float32)[0])
    n_rows = NTOK // GRID_W
    pos = np.stack([np.repeat(np.arange(n_rows), GRID_W), np.tile(np.arange(GRID_W), n_rows)], -1).astype(np.float32)
    inv_freq = (10000.0 ** (-np.arange(0, 32, 2, dtype=np.float32) / 32)).astype(np.float32)
    ang = pos[:, :, None] * inv_freq
    cos, sin = np.cos(ang).astype(np.float32), np.sin(ang).astype(np.float32)
    dd = np.arange(128) % 64
    a_i, h_i, j_i = dd // 32, (dd % 32) // 16, dd % 16
    cosF = cos[:, a_i, j_i].T
    sinF = (sin[:, a_i, j_i] * np.where(h_i == 0, -1.0, 1.0)[None, :]).T
    colv_common = np.zeros((128, 40), np.float32)
    for i, nm in enumerate(['conv_b', 'conv_ln_g', 'conv_ln_b']):
        colv_common[:, i * 8:(i + 1) * 8] = np.asarray(inputs[nm], np.float32)[0].reshape(8, 128).T
    colv_common[:, 24] = np.asarray(inputs['subln_g'], np.float32)[0]
    conv_w = np.ascontiguousarray(np.asarray(inputs['conv_w'], np.float32)[0].reshape(CONV_K, 8, 128).transpose(2, 1, 0))
    shared = {
        'w_ada': np.ascontiguousarray(np.asarray(inputs['w_ada'], np.float32)[0]),
        'b_ada': np.ascontiguousarray(np.asarray(inputs['b_ada'], np.float32)[0][None, :]),
        'norm_g': np.ascontiguousarray(np.asarray(inputs['norm_g'], np.float32)[0]),
        'ffn1_w13': np.ascontiguousarray(np.asarray(inputs['ffn1_w13'], np.float32)[0]),
        'ffn2_w13': np.ascontiguousarray(np.asarray(inputs['ffn2_w13'], np.float32)[0]),
        'ffn1_w2': np.ascontiguousarray(np.asarray(inputs['ffn1_w2'], np.float32)[0]),
        'ffn2_w2': np.ascontiguousarray(np.asarray(inputs['ffn2_w2'], np.float32)[0]),
        'w_in': w_in,
        'lambda_qk': np.ascontiguousarray(np.asarray(inputs['lambda_qk'], np.float32)[0].reshape(1, 256)),
        'conv_w': conv_w,
        'w_attn_proj': np.ascontiguousarray(np.asarray(inputs['w_attn_proj'], np.float32)[0]),
        'w_conv_proj': np.ascontiguousarray(np.asarray(inputs['w_conv_proj'], np.float32)[0]),
        'w_out': np.ascontiguousarray(np.asarray(inputs['w_out'], np.float32)[0]),
    }
    maps = []
    for ci in range(n_cores):
        b, h = batch_of(ci), half_of(ci)
        own = slice(h * NO, (h + 1) * NO)
        oth = slice((1 - h) * NO, (2 - h) * NO)
        m = dict(shared)
        m['xc'] = np.concatenate([x[b, own], x[b, oth], ctx[b]], 0)
        cv = np.zeros((128, 16), np.float32)
        cv[:, 0:8] = c[b].reshape(8, 128).T
        cv[:, 8:16] = c_ctx.reshape(8, 128).T
        m['cvec'] = cv
        colv = colv_common.copy()
        colv[:, 25] = 1.0 if h == 1 else 0.0
        colv[:, 26] = 1.0 if h == 0 else 0.0
        m['colv'] = colv
        ct = np.ones((128, NALL), np.float32); sn = np.zeros((128, NALL), np.float32)
        ct[:, :NO] = cosF[:, own]; ct[:, NO:2 * NO] = cosF[:, oth]
        sn[:, :NO] = sinF[:, own]; sn[:, NO:2 * NO] = sinF[:, oth]
        m['cosT'] = ct; m['sinT'] = sn
        maps.append(m)
    return maps


_NC_CACHE = {}


def kernel(**inputs):
    NO = 4096
    if NO not in _NC_CACHE:
        _NC_CACHE[NO] = build(NO)
    nc = _NC_CACHE[NO]
    maps = _host_prep(inputs, NO, 8, lambda ci: ci // 2, lambda ci: ci % 2)
    res = run_bass_kernel_spmd(nc, maps, core_ids=list(range(8)))
    outp = np.empty((4, 2 * NO, D), np.float32)
    for ci in range(8):
        outp[ci // 2, (ci % 2) * NO:(ci % 2 + 1) * NO] = res.results[ci]["out"]
    return outp
```

```python
import math
from contextlib import ExitStack
import numpy as np
import concourse.bass as bass
import concourse.mybir as mybir
from concourse.bass_utils import run_bass_kernel_spmd

F32 = mybir.dt.float32
BF16 = mybir.dt.bfloat16
AF = mybir.ActivationFunctionType
ALU = mybir.AluOpType

D = 1024
DFF = 2816
NF = DFF // 128
CTX = 256
CONV_K = 31
EPS = 1e-6
LAM_INIT = 0.2
GRID_W = 64


class Trk:
    def __init__(self, nc, es):
        self.nc = nc
        self.es = es
        self.E = {'pe': nc.tensor, 'act': nc.scalar, 'dve': nc.vector, 'pool': nc.gpsimd, 'sp': nc.sync}
        self.sem = {k: es.enter_context(nc.semaphore('s_' + k)) for k in ['pe', 'act', 'dve', 'pool']}
        self.cnt = {k: 0 for k in self.sem}
        self.dsem = {}
        self.waited = {e: {} for e in self.E}
        self.res = {}

    def _semobj(self, sk):
        return self.sem[sk] if sk in self.sem else self.dsem[sk][0]

    def _need(self, reads, writes):
        best = {}

        def add(sk, v):
            if v > best.get(sk, 0):
                best[sk] = v
        for r in reads:
            e = self.res.get(r)
            if e and e[0]:
                add(*e[0])
        for w in writes:
            e = self.res.get(w)
            if e:
                if e[0]:
                    add(*e[0])
                for sk, v in e[1].items():
                    add(sk, v)
        return best

    def _wait(self, eng, best):
        for sk, v in best.items():
            if eng == 'pe' and sk == 'pe':
                continue
            if self.waited[eng].get(sk, 0) >= v:
                continue
            self.E[eng].wait_ge(self._semobj(sk), v)
            self.waited[eng][sk] = v

    def _update(self, tok, reads, writes):
        for r in reads:
            e = self.res.setdefault(r, [None, {}])
            if tok[1] > e[1].get(tok[0], 0):
                e[1][tok[0]] = tok[1]
        for w in writes:
            self.res[w] = [tok, {}]

    def op(self, eng, reads, writes, fn):
        self._wait(eng, self._need(reads, writes))
        ins = fn()
        self.cnt[eng] += 1
        ins.then_inc(self.sem[eng], 1)
        self._update((eng, self.cnt[eng]), reads, writes)

    def dma(self, q, semname, reads, writes, pairs):
        if semname not in self.dsem:
            self.dsem[semname] = [self.es.enter_context(self.nc.semaphore('d_' + semname)), 0]
        self._wait(q, self._need(reads, writes))
        d = self.dsem[semname]
        for (o, i) in pairs:
            self.E[q].dma_start(out=o, in_=i).then_inc(d[0], 16)
            d[1] += 16
        self._update((semname, d[1]), reads, writes)

    def wait_all(self, eng, keys):
        self._wait(eng, self._need(keys, keys))


def build(NO, dbg=False):
    NALL = 2 * NO + CTX
    NKT = NALL // 128
    NTO = NO // 512
    nc = bass.Bass("TRN2", target_bir_lowering=False)

    def din(name, shape, dt=F32):
        return nc.dram_tensor(name, shape, dt, kind="ExternalInput").ap()

    def dscr(name, shape, dt):
        return nc.dram_tensor(name, shape, dt, kind="ExternalOutput" if dbg else "Internal").ap()

    xc = din("xc", [NALL, D])
    cvec = din("cvec", [128, 16])
    w_ada = din("w_ada", [D, 9 * D])
    b_ada = din("b_ada", [1, 9 * D])
    norm_g = din("norm_g", [6, D])
    w13 = [din("ffn1_w13", [D, 2 * DFF]), din("ffn2_w13", [D, 2 * DFF])]
    w2 = [din("ffn1_w2", [DFF, D]), din("ffn2_w2", [DFF, D])]
    w_in = din("w_in", [D, 7 * D])
    lam_qk = din("lambda_qk", [1, 256])
    colv = din("colv", [128, 40])
    conv_w = din("conv_w", [128, 8, CONV_K])
    w_ap = din("w_attn_proj", [D, D])
    w_cp = din("w_conv_proj", [D, D])
    w_o = din("w_out", [D, D])
    cosT = din("cosT", [128, NALL])
    sinT = din("sinT", [128, NALL])
    out = nc.dram_tensor("out", [NO, D], F32, kind="ExternalOutput").ap()

    TAB = dscr("TAB", [14, 128, D], F32)
    X1 = dscr("X1", [NALL, D], F32)
    X2 = dscr("X2", [NO, D], F32)
    KT = dscr("KT", [8, 128, NALL], BF16)
    QT = dscr("QT", [8, 128, NO], BF16)
    V = dscr("V", [NALL, D], BF16)
    U = dscr("U", [128, 8, NO + 1024], BF16)
    SGA = dscr("SGA", [128, 8, NO], BF16)
    SGB = dscr("SGB", [128, 8, NO], BF16)
    OT = dscr("OT", [128, 8, NO], BF16)

    kc = lambda w: w.rearrange("(c p) n -> p c n", p=128)

    with ExitStack() as top:
        T = Trk(nc, top)
        _uid = [0]

        def sbt(es, n, s, d):
            _uid[0] += 1
            return es.enter_context(nc.sbuf_tensor(f"{n}_{_uid[0]}", s, d))
        ident = sbt(top, "ident", [128, 128], BF16)
        onesb = sbt(top, "onesb", [128, 128], BF16)
        onesf = sbt(top, "onesf", [128, 128], F32)
        epsc = sbt(top, "epsc", [128, 1], F32)
        cols = sbt(top, "cols", [128, 40], F32)
        lamt = sbt(top, "lamt", [128, 8], F32)
        P = [top.enter_context(nc.psum_tensor(f"P{i}", [128, 1024], F32)) for i in range(4)]

        T.op('pool', [], ['ident'], lambda: nc.gpsimd.memset(ident[:], 1.0))
        T.op('pool', ['ident'], ['ident'], lambda: nc.gpsimd.affine_select(
            out=ident[:], in_=ident[:], pattern=[[-1, 128]], compare_op=ALU.is_equal, fill=0.0,
            base=0, channel_multiplier=1))
        T.op('dve', [], ['onesb'], lambda: nc.vector.memset(onesb[:], 1.0))
        T.op('dve', [], ['onesf'], lambda: nc.vector.memset(onesf[:], 1.0))
        T.op('dve', [], ['epsc'], lambda: nc.vector.memset(epsc[:], EPS))
        T.dma('sp', 'ld_cols', [], ['cols'], [(cols[:], colv[:, :])])

        with ExitStack() as es:
            cv = sbt(es, "cv", [128, 16], F32)
            cs = sbt(es, "cs", [128, 16], F32)
            rep = sbt(es, "rep", [128, 16, 128], BF16)
            wa = sbt(es, "wa", [128, 8, 2048], BF16)
            ba = sbt(es, "ba", [1, 9 * D], BF16)
            gbc = sbt(es, "gbc", [128, 6, D], F32)
            lq = sbt(es, "lq", [128, 256], F32)
            lqp = sbt(es, "lqp", [128, 128], F32)
            mt = [sbt(es, f"mt{i}", [128, 9, D], F32) for i in range(2)]
            tb = sbt(es, "tb", [128, 2, D], F32)
            T.dma('sp', 'misc', [], ['cv', 'gbc', 'lq'],
                  [(cv[:], cvec[:, :]), (lq[:], lam_qk[0:1, :].partition_broadcast(128))] +
                  [(gbc[:, i, :], norm_g[i:i + 1, :].partition_broadcast(128)) for i in range(6)])
            T.dma('pool', 'wts', [], ['ba'], [(ba[:], b_ada[:, :])])
            T.op('dve', ['lq'], ['lqp'], lambda: nc.vector.tensor_tensor(
                out=lqp[:].rearrange("p (a d) -> p a d", a=2),
                in0=lq[:].rearrange("p (a b d) -> p a b d", a=2, b=2)[:, :, 0, :],
                in1=lq[:].rearrange("p (a b d) -> p a b d", a=2, b=2)[:, :, 1, :], op=ALU.mult))
            T.op('dve', ['lqp'], ['lam0'], lambda: nc.vector.reduce_sum(
                out=lamt[:, 0:2], in_=lqp[:].rearrange("p (a d) -> p a d", a=2), axis=mybir.AxisListType.X))
            T.op('act', ['lam0'], ['lam1'], lambda: nc.scalar.activation(out=lamt[:, 2:4], in_=lamt[:, 0:2], func=AF.Exp))
            T.op('dve', ['lam1'], ['lam2'], lambda: nc.vector.tensor_tensor(
                out=lamt[:, 4:5], in0=lamt[:, 3:4], in1=lamt[:, 2:3], op=ALU.subtract))
            T.op('dve', ['lam2'], ['neglam'], lambda: nc.vector.tensor_scalar(
                out=lamt[:, 5:6], in0=lamt[:, 4:5], scalar1=-LAM_INIT, scalar2=None, op0=ALU.add))
            T.op('dve', ['cols'], ['sublc'], lambda: nc.vector.tensor_scalar(
                out=lamt[:, 6:7], in0=cols[:, 24:25], scalar1=1.0 - LAM_INIT, scalar2=None, op0=ALU.mult))
            T.op('act', ['cv'], ['cs'], lambda: nc.scalar.activation(out=cs[:], in_=cv[:], func=AF.Silu))
            for j in range(16):
                T.op('dve', ['cs', 'onesb'], ['rep'], lambda j=j: nc.vector.tensor_scalar(
                    out=rep[:, j, :], in0=onesb[:], scalar1=cs[:, j:j + 1], scalar2=None, op0=ALU.mult))
            wav = kc(w_ada)
            for k9 in range(9):
                sl = k9 % 2
                T.dma('pool', f'wa{sl}', [], [f'wa{sl}'],
                      [(wa[:, c, sl * 1024:(sl + 1) * 1024], wav[:, c, k9 * 1024:(k9 + 1) * 1024]) for c in range(8)])
                for which in range(2):
                    pp = P[which * 2 + 0] if True else None
                    pp = P[which]

                    def mm(which=which, sl=sl, k9=k9, pp=pp):
                        ins = None
                        for n in range(2):
                            for c in range(8):
                                nc.tensor.matmul(pp[:, n * 512:(n + 1) * 512], rep[:, which * 8 + c, :],
                                                 wa[:, c, sl * 1024 + n * 512: sl * 1024 + (n + 1) * 512],
                                                 start=(c == 0), stop=False)
                            ins = nc.tensor.matmul(pp[:, n * 512:(n + 1) * 512], onesb[0:1, :],
                                                   ba[0:1, k9 * 1024 + n * 512:k9 * 1024 + (n + 1) * 512],
                                                   start=False, stop=True)
                        return ins
                    T.op('pe', ['rep', f'wa{sl}', 'ba', 'onesb'], [f'P{which}'], mm)
                    T.op('act', [f'P{which}'], [f'mt{which}_{k9}'], lambda which=which, k9=k9, pp=pp:
                         nc.scalar.copy(out=mt[which][:, k9, :], in_=pp[:, :]))
            def mk_tab(idx, which, kind, k, gi, mult):
                m = mt[which]
                sl = idx % 2
                if kind == 'A':
                    T.op('dve', [f'mt{which}_{k + 1}', 'gbc'], [f'tb{sl}'], lambda: nc.vector.scalar_tensor_tensor(
                        out=tb[:, sl, :], in0=m[:, k + 1, :], scalar=1.0, in1=gbc[:, gi, :], op0=ALU.add, op1=ALU.mult))
                elif kind == 'B':
                    T.op('dve', [f'mt{which}_{k}'], [f'tb{sl}'], lambda: nc.vector.tensor_copy(tb[:, sl, :], m[:, k, :]))
                else:
                    T.op('dve', [f'mt{which}_{k + 2}', 'gbc'], [f'tb{sl}'], lambda: nc.vector.scalar_tensor_tensor(
                        out=tb[:, sl, :], in0=m[:, k + 2, :], scalar=mult, in1=gbc[:, gi, :], op0=ALU.mult, op1=ALU.mult))
                T.dma('sp', f'tbst{sl}', [f'tb{sl}'], [f'TAB{idx}'], [(TAB[idx], tb[:, sl, :])])
            specs = [(0, 'A', 0, 0, 0), (0, 'B', 0, 0, 0), (0, 'G', 0, 1, 0.5),
                     (0, 'A', 3, 2, 0), (0, 'B', 3, 0, 0), (0, 'G', 3, 3, 1.0),
                     (0, 'A', 6, 4, 0), (0, 'B', 6, 0, 0), (0, 'G', 6, 5, 0.5),
                     (1, 'A', 0, 0, 0), (1, 'B', 0, 0, 0), (1, 'G', 0, 1, 0.5),
                     (1, 'A', 3, 2, 0), (1, 'B', 3, 0, 0)]
            for idx, (which, kind, k, gi, mult) in enumerate(specs):
                mk_tab(idx, which, kind, k, gi, mult)
            skeys = ([f'TAB{i}' for i in range(14)] + ['wa0', 'wa1', 'ba', 'rep', 'gbc', 'lq', 'lqp', 'tb0', 'tb1', 'cs', 'cv', 'P0', 'P1']
                     + [f'mt{w}_{k}' for w in range(2) for k in range(9)])
            for e in ['pe', 'act', 'dve', 'pool', 'sp']:
                T.wait_all(e, skeys)

        def rms_prenorm(es_bufs, xs, xs_key, At, Bt, tabkeys, tag, s, junk_key='junk'):
            junk, st, tmp, hb = es_bufs
            c0 = (s % 2) * 4
            T.op('act', [xs_key], [junk_key, f'st{c0}'], lambda: nc.scalar.activation(
                out=junk[:], in_=xs, func=AF.Square, accum_out=st[:, c0:c0 + 1]))
            T.op('act', [f'st{c0}', 'epsc'], [f'st{c0 + 1}'], lambda: nc.scalar.activation(
                out=st[:, c0 + 1:c0 + 2], in_=st[:, c0:c0 + 1], func=AF.Sqrt, scale=1.0 / D, bias=epsc[:, 0:1]))
            T.op('dve', [f'st{c0 + 1}'], [f'st{c0 + 2}'], lambda: nc.vector.reciprocal(
                out=st[:, c0 + 2:c0 + 3], in_=st[:, c0 + 1:c0 + 2]))
            T.op('dve', [xs_key, f'st{c0 + 2}'] + tabkeys, ['tmp'], lambda: nc.vector.scalar_tensor_tensor(
                out=tmp[:], in0=xs, scalar=st[:, c0 + 2:c0 + 3], in1=At, op0=ALU.mult, op1=ALU.mult))
            T.op('dve', ['tmp'] + tabkeys, ['hb'], lambda: nc.vector.tensor_tensor(
                out=hb[:], in0=tmp[:], in1=Bt, op=ALU.add))

        def transp_to(hb, hT, s, nsub_key):
            half = s % 2
            pv = P[3][:].bitcast(BF16)
            def tr():
                ins = None
                for c in range(8):
                    ins = nc.tensor.transpose(pv[:, half * 1024 + c * 128: half * 1024 + (c + 1) * 128],
                                              hb[:, c * 128:(c + 1) * 128], ident[:])
                return ins
            T.op('pe', ['hb', 'ident'], [f'P3{half}'], tr)
            T.op('act', [f'P3{half}'], [nsub_key], lambda: nc.scalar.copy(
                out=hT[:, :, s * 128:(s + 1) * 128],
                in_=pv[:, half * 1024:(half + 1) * 1024].rearrange("p (c t) -> p c t", c=8)))

        def ffn_phase(wi, src, dst, tab_lat, tab_ctx, tiles):
            with ExitStack() as es:
                w13s = sbt(es, "w13s", [128, 8, 2 * DFF], BF16)
                w2s = sbt(es, "w2s", [128, NF, D], BF16)
                tabAB = sbt(es, "tabAB", [128, 2, D], F32)
                tabG = sbt(es, "tabG", [128, D], F32)
                xb = [sbt(es, f"xb{i}", [128, D], F32) for i in range(2)]
                hT2 = [sbt(es, f"hT{i}", [128, 8, 512], BF16) for i in range(2)]
                actT = sbt(es, "actT", [128, NF, 512], BF16)
                junk = sbt(es, "junk", [128, D], BF16)
                st = sbt(es, "st", [128, 16], F32)
                tmp = sbt(es, "tmp", [128, D], F32)
                hb = sbt(es, "hb", [128, D], BF16)
                sg = [sbt(es, f"sg{i}", [128, 512], F32) for i in range(2)]
                yb = sbt(es, "yb", [128, D], F32)
                w13v = kc(w13[wi])
                w2v = w2[wi].rearrange("(f p) n -> p f n", p=128)
                FG = [(0, 4), (4, 10), (10, 16), (16, NF)]
                fgrp = {}
                for g, (fs, fe) in enumerate(FG):
                    for f in range(fs, fe):
                        fgrp[f] = g
                    T.dma('pool', f'wtsg{g}', [], [f'w13s{g}'],
                          [(w13s[:, c, h * DFF + fs * 128: h * DFF + fe * 128], w13v[:, c, h * DFF + fs * 128: h * DFF + fe * 128])
                           for c in range(8) for h in range(2)])
                T.dma('pool', 'wts', [], ['w2s'], [(w2s[:, f, :], w2v[:, f, :]) for f in range(NF)])
                state = {'ab': None, 'g': None, 'xcnt': 0}

                def tabs_of(is_ctx):
                    return tab_ctx if is_ctx else tab_lat

                def prenorm_sub(i, s):
                    tok0, nsub, is_ctx = tiles[i]
                    tset = tabs_of(is_ctx)
                    if state['ab'] != tset:
                        T.dma('sp', 'tabs', [], ['tabAB'], [(tabAB[:, j, :], TAB[tset[j]]) for j in range(2)])
                        state['ab'] = tset
                    b = state['xcnt'] % 2; state['xcnt'] += 1
                    T.dma('sp', f'x{b}', [], [f'xb{b}'], [(xb[b][:], src[tok0 + s * 128: tok0 + (s + 1) * 128, :])])
                    rms_prenorm((junk, st, tmp, hb), xb[b][:], f'xb{b}', tabAB[:, 0, :], tabAB[:, 1, :], ['tabAB'], 'f', s)

                def transp_sub(i, s):
                    transp_to(hb, hT2[i % 2], s, f'hT{i % 2}')

                for s in range(tiles[0][1]):
                    prenorm_sub(0, s)
                    transp_sub(0, s)
                for i, (tok0, nsub, is_ctx) in enumerate(tiles):
                    NT = nsub * 128
                    hT = hT2[i % 2]; hk = f'hT{i % 2}'
                    nxt = tiles[i + 1] if i + 1 < len(tiles) else None
                    for f in range(NF):
                        if nxt is not None:
                            if f % 5 == 1 and f // 5 < nxt[1]:
                                prenorm_sub(i + 1, f // 5)
                            if f % 5 == 4 and f // 5 < nxt[1]:
                                transp_sub(i + 1, f // 5)
                        pp = P[f % 2]
                        def up(f=f, pp=pp):
                            ins = None
                            for half, col0 in ((0, f * 128), (1, DFF + f * 128)):
                                for c in range(8):
                                    ins = nc.tensor.matmul(pp[:, half * 512: half * 512 + NT], w13s[:, c, col0:col0 + 128],
                                                           hT[:, c, 0:NT], start=(c == 0), stop=(c == 7))
                            return ins
                        T.op('pe', [f'w13s{fgrp[f]}', hk], [f'P{f % 2}'], up)
                        T.op('act', [f'P{f % 2}'], [f'sg{f % 2}'], lambda f=f, pp=pp: nc.scalar.activation(
                            out=sg[f % 2][:, 0:NT], in_=pp[:, 0:NT], func=AF.Silu))
                        T.op('dve', [f'sg{f % 2}', f'P{f % 2}'], ['actT'], lambda f=f, pp=pp: nc.vector.tensor_tensor(
                            out=actT[:, f, 0:NT], in0=sg[f % 2][:, 0:NT], in1=pp[:, 512:512 + NT], op=ALU.mult))
                    tset = tabs_of(is_ctx)
                    if state['g'] != tset:
                        T.dma('sp', 'tabs', [], ['tabG'], [(tabG[:], TAB[tset[2]])])
                        state['g'] = tset
                    for s in range(nsub):
                        def down(s=s):
                            ins = None
                            for n in range(2):
                                for f in range(NF):
                                    ins = nc.tensor.matmul(P[2][:, n * 512:(n + 1) * 512], actT[:, f, s * 128:(s + 1) * 128],
                                                           w2s[:, f, n * 512:(n + 1) * 512], start=(f == 0), stop=(f == NF - 1))
                            return ins
                        T.op('pe', ['w2s', 'actT'], ['P2'], down)
                        b = state['xcnt'] % 2; state['xcnt'] += 1
                        T.dma('sp', f'x{b}', [], [f'xb{b}'], [(xb[b][:], src[tok0 + s * 128: tok0 + (s + 1) * 128, :])])
                        c0 = 8 + (s % 2) * 4
                        T.op('act', ['P2'], ['yb'], lambda: nc.scalar.copy(out=yb[:], in_=P[2][:, :]))
                        T.op('act', ['yb'], ['junk', f'st{c0}'], lambda c0=c0: nc.scalar.activation(
                            out=junk[:], in_=yb[:], func=AF.Square, accum_out=st[:, c0:c0 + 1]))
                        T.op('act', [f'st{c0}', 'epsc'], [f'st{c0 + 1}'], lambda c0=c0: nc.scalar.activation(
                            out=st[:, c0 + 1:c0 + 2], in_=st[:, c0:c0 + 1], func=AF.Sqrt, scale=1.0 / D, bias=epsc[:, 0:1]))
                        T.op('dve', [f'st{c0 + 1}'], [f'st{c0 + 2}'], lambda c0=c0: nc.vector.reciprocal(
                            out=st[:, c0 + 2:c0 + 3], in_=st[:, c0 + 1:c0 + 2]))
                        T.op('dve', ['yb', f'st{c0 + 2}', 'tabG'], ['yb'], lambda c0=c0: nc.vector.scalar_tensor_tensor(
                            out=yb[:], in0=yb[:], scalar=st[:, c0 + 2:c0 + 3], in1=tabG[:], op0=ALU.mult, op1=ALU.mult))
                        T.op('dve', ['yb', f'xb{b}'], [f'xb{b}'], lambda b=b: nc.vector.tensor_tensor(
                            out=xb[b][:], in0=yb[:], in1=xb[b][:], op=ALU.add))
                        T.dma('pool', f'xst{b}', [f'xb{b}'], [dst.tensor.name + str(tok0 + s * 128)],
                              [(dst[tok0 + s * 128: tok0 + (s + 1) * 128, :], xb[b][:])])
                phase_end(['w13s0', 'w13s1', 'w13s2', 'w13s3', 'w2s', 'tabAB', 'tabG', 'xb0', 'xb1', 'hT0', 'hT1', 'actT', 'junk', 'tmp', 'hb', 'sg0', 'sg1', 'yb']
                          + [f'st{i}' for i in range(16)])

        def phase_end(keys):
            allk = keys + ['P0', 'P1', 'P2', 'P30', 'P31']
            for e in ['pe', 'act', 'dve', 'pool', 'sp']:
                T.wait_all(e, allk)

        tilesA = [(i * 512, 4, False) for i in range(2 * NO // 512)] + [(2 * NO, 2, True)]
        ffn_phase(0, xc, X1, (0, 1, 2), (9, 10, 11), tilesA)
        for i_ in range(2):
            T.res[f'X1all{i_}'] = [(f'xst{i_}', T.dsem[f'xst{i_}'][1]), {}]

        def proj_phase():
            with ExitStack() as es:
                wq = sbt(es, "wq", [128, 8, 7 * D], BF16)
                qa = [sbt(es, f"qa{i}", [128, 512], BF16) for i in range(2)]
                perm = sbt(es, "perm", [128, 128], BF16)
                tabs = sbt(es, "tabs2", [128, 2, D], F32)
                xb = [sbt(es, f"xb{i}", [128, D], F32) for i in range(2)]
                hT2 = [sbt(es, f"hT{i}", [128, 8, 512], BF16) for i in range(2)]
                st = sbt(es, "st", [128, 8], F32)
                tmp = sbt(es, "tmp", [128, D], F32)
                hb = sbt(es, "hb", [128, D], BF16)
                cst = sbt(es, "cst", [128, 2, 512], F32)
                t1 = [sbt(es, f"t1{i}", [128, 512], F32) for i in range(2)]
                t2 = [sbt(es, f"t2{i}", [128, 512], F32) for i in range(2)]
                kst2 = [sbt(es, f"kst{i}", [128, 8, 512], BF16) for i in range(2)]
                kcnt = [0]
                vst = sbt(es, "vst", [128, D], BF16)
                sgt1 = sbt(es, "sgt", [128, 512], F32)
                sgt = [sgt1, sgt1]
                win_v = kc(w_in)
                iv = ident[:].rearrange("p (b h j) -> p b h j", b=4, h=2)
                pv_ = perm[:].rearrange("p (b h j) -> p b h j", b=4, h=2)
                def mkperm():
                    nc.vector.tensor_copy(pv_[:, :, 0, :], iv[:, :, 1, :])
                    return nc.vector.tensor_copy(pv_[:, :, 1, :], iv[:, :, 0, :])
                T.op('dve', ['ident'], ['perm'], mkperm)
                for (wt_, wv_, nm, blk) in ((wq, win_v, 'wq', 1), (wq, win_v, 'wq', 2), (wq, win_v, 'wq', 0),
                                            (wq, win_v, 'wq', 3), (wq, win_v, 'wq', 4), (wq, win_v, 'wq', 5),
                                            (wq, win_v, 'wq', 6)):
                    T.dma('pool', f'wts_{nm}{blk}', [], [f'{nm}{blk}'],
                          [(wt_[:, c, blk * D:(blk + 1) * D], wv_[:, c, blk * D:(blk + 1) * D]) for c in range(8)])
                state = {'tab': None, 'xcnt': 0}
                pcnt = [0]
                kq_cnt = [0]
                tiles = [(i * 512, 4, False, i < NTO) for i in range(2 * NTO)] + [(2 * NO, 2, True, False)]

                def prenorm_sub(ti, s):
                    tok0, nsub, is_ctx, own = tiles[ti]
                    tset = (12, 13) if is_ctx else (3, 4)
                    if state['tab'] != tset:
                        T.dma('sp', 'tabs', [], ['tabs'], [(tabs[:, i, :], TAB[tset[i]]) for i in range(2)])
                        state['tab'] = tset
                    b = state['xcnt'] % 2; state['xcnt'] += 1
                    T.dma('sp', f'x{b}', ['X1all0', 'X1all1'], [f'xb{b}'], [(xb[b][:], X1[tok0 + s * 128: tok0 + (s + 1) * 128, :])])
                    rms_prenorm((hb, st, tmp, hb), xb[b][:], f'xb{b}', tabs[:, 0, :], tabs[:, 1, :], ['tabs'], 'p', s, junk_key='hb')

                def transp_sub(ti, s):
                    transp_to(hb, hT2[ti % 2], s, f'hT{ti % 2}')

                for s in range(tiles[0][1]):
                    prenorm_sub(0, s)
                    transp_sub(0, s)
                tick = [0]
                for ti, (tok0, nsub, is_ctx, own) in enumerate(tiles):
                    NT = nsub * 128
                    hT = hT2[ti % 2]; hk = f'hT{ti % 2}'
                    nxt = tiles[ti + 1] if ti + 1 < len(tiles) else None
                    tick[0] = 0
                    T.dma('sp', 'cst', [], ['cst'], [(cst[:, 0, 0:NT], cosT[:, tok0:tok0 + NT]), (cst[:, 1, 0:NT], sinT[:, tok0:tok0 + NT])])

                    def pe_tick():
                        k = tick[0]; tick[0] += 1
                        if nxt is None:
                            return
                        if k % 3 == 0 and k // 3 < nxt[1]:
                            prenorm_sub(ti + 1, k // 3)
                        if k % 3 == 2 and k // 3 < nxt[1]:
                            transp_sub(ti + 1, k // 3)

                    def pair_mm(wa_, ca, wb_, cb):
                        pe_tick()
                        i = pcnt[0] % 2; pcnt[0] += 1
                        pp = P[i]
                        def mm():
                            ins = None
                            for half, (w_, c0) in enumerate(((wa_, ca), (wb_, cb))):
                                for c in range(8):
                                    ins = nc.tensor.matmul(pp[:, half * 512: half * 512 + NT], w_[:, c, c0:c0 + 128],
                                                           hT[:, c, 0:NT], start=(c == 0), stop=(c == 7))
                            return ins
                        ka = 'wq' + str(ca // D)
                        kb = 'wq' + str(cb // D)
                        T.op('pe', [ka, kb, hk], [f'P{i}a', f'P{i}b'], mm)
                        return i

                    def rope_proj(col0, dstT, dcol0):
                        pend = []
                        kb = kcnt[0] % 2; kcnt[0] += 1
                        kst = kst2[kb]; kk = f'kst{kb}'

                        def finish(p, i, j):
                            def rmm():
                                return nc.tensor.matmul(P[i][:, 512:512 + NT], perm[:], qa[j][:, 0:NT], start=True, stop=True)
                            T.op('pe', [f'qa{j}', 'perm'], [f'P{i}b'], rmm)
                            T.op('dve', [f'P{i}a', f'qa{j}', 'cst'], [f't1{j}'], lambda: nc.vector.tensor_tensor(
                                out=t1[j][:, 0:NT], in0=P[i][:, 0:NT], in1=cst[:, 0, 0:NT], op=ALU.mult))
                            T.op('dve', [f'P{i}b', 'cst'], [f't2{j}'], lambda: nc.vector.tensor_tensor(
                                out=t2[j][:, 0:NT], in0=P[i][:, 512:512 + NT], in1=cst[:, 1, 0:NT], op=ALU.mult))
                            T.op('pool', [f't1{j}', f't2{j}'], [kk], lambda: nc.gpsimd.tensor_tensor(
                                out=kst[:, p, 0:NT], in0=t1[j][:, 0:NT], in1=t2[j][:, 0:NT], op=ALU.add))
                        for p in range(8):
                            pe_tick()
                            i = pcnt[0] % 2; pcnt[0] += 1
                            j = kq_cnt[0] % 2; kq_cnt[0] += 1
                            def mm(i=i, p=p):
                                ins = None
                                for c in range(8):
                                    ins = nc.tensor.matmul(P[i][:, 0:NT], wq[:, c, col0 + p * 128: col0 + (p + 1) * 128],
                                                           hT[:, c, 0:NT], start=(c == 0), stop=(c == 7))
                                return ins
                            T.op('pe', ['wq' + str(col0 // D), hk], [f'P{i}a'], mm)
                            T.op('act', [f'P{i}a'], [f'qa{j}'], lambda i=i, j=j: nc.scalar.copy(out=qa[j][:, 0:NT], in_=P[i][:, 0:NT]))
                            if pend:
                                finish(*pend.pop())
                            pend.append((p, i, j))
                        finish(*pend.pop())
                        T.dma('pool', kk, [kk], [dstT.tensor.name + str(dcol0)],
                              [(dstT.rearrange("g p t -> p g t")[:, :, dcol0:dcol0 + NT], kst[:, :, 0:NT])])

                    def glu_proj(ucol0, mask_col):
                        kb = kcnt[0] % 2; kcnt[0] += 1
                        kst = kst2[kb]; kk = f'kst{kb}'
                        for c8 in range(8):
                            i = pair_mm(wq, 3 * D + c8 * 128, wq, 4 * D + c8 * 128)
                            j = 0
                            T.op('act', [f'P{i}b'], [f'sgt{j}'], lambda i=i, j=j: nc.scalar.activation(
                                out=sgt[j][:, 0:NT], in_=P[i][:, 512:512 + NT], func=AF.Sigmoid))
                            if mask_col is None:
                                T.op('dve', [f'sgt{j}', f'P{i}a'], [kk], lambda i=i, j=j, c8=c8: nc.vector.tensor_tensor(
                                    out=kst[:, c8, 0:NT], in0=sgt[j][:, 0:NT], in1=P[i][:, 0:NT], op=ALU.mult))
                            else:
                                T.op('dve', [f'sgt{j}', f'P{i}a', 'cols'], [kk], lambda i=i, j=j, c8=c8: nc.vector.scalar_tensor_tensor(
                                    out=kst[:, c8, 0:NT], in0=sgt[j][:, 0:NT], scalar=cols[:, mask_col:mask_col + 1],
                                    in1=P[i][:, 0:NT], op0=ALU.mult, op1=ALU.mult))
                        T.dma('pool', kk, [kk], ['U' + str(ucol0)], [(U[:, :, ucol0:ucol0 + NT], kst[:, :, 0:NT])])

                    def gate_proj(col0, dstT):
                        kb = kcnt[0] % 2; kcnt[0] += 1
                        kst = kst2[kb]; kk = f'kst{kb}'
                        for c8 in range(0, 8, 2):
                            i = pair_mm(wq, col0 + c8 * 128, wq, col0 + (c8 + 1) * 128)
                            T.op('act', [f'P{i}a', f'P{i}b'], [kk], lambda i=i, c8=c8: nc.scalar.activation(
                                out=kst[:, c8:c8 + 2, 0:NT], in_=P[i][:].rearrange("p (a t) -> p a t", a=2)[:, :, 0:NT], func=AF.Sigmoid))
                        T.dma('pool', kk, [kk], [dstT.tensor.name + str(tok0)], [(dstT[:, :, tok0:tok0 + NT], kst[:, :, 0:NT])])

                    rope_proj(D, KT, tok0)
                    for s in range(nsub):
                        def vmm(s=s):
                            ins = None
                            for n in range(2):
                                for c in range(8):
                                    ins = nc.tensor.matmul(P[2][:, n * 512:(n + 1) * 512], hT[:, c, s * 128:(s + 1) * 128],
                                                           wq[:, c, 2 * D + n * 512: 2 * D + (n + 1) * 512], start=(c == 0), stop=(c == 7))
                            return ins
                        pe_tick()
                        T.op('pe', ['wq2', hk], ['P2'], vmm)
                        T.op('act', ['P2'], ['vst'], lambda: nc.scalar.copy(out=vst[:], in_=P[2][:, :]))
                        T.dma('pool', 'vst', ['vst'], ['V' + str(tok0 + s * 128)], [(V[tok0 + s * 128: tok0 + (s + 1) * 128, :], vst[:])])
                    if own:
                        rope_proj(0, QT, tok0)
                        glu_proj(512 + tok0, None)
                        gate_proj(5 * D, SGA)
                        gate_proj(6 * D, SGB)
                    elif ti == NTO:
                        glu_proj(512 + NO, 26)
                    if ti == 2 * NTO - 1:
                        glu_proj(0, 25)
                phase_end([f'wq{i}' for i in range(7)] + ['qa0', 'qa1', 'perm', 'P0a', 'P0b', 'P1a', 'P1b', 'tabs', 'xb0', 'xb1', 'hT0', 'hT1', 'tmp', 'hb', 'cst', 't10', 't11', 't20', 't21',
                           'kst0', 'kst1', 'vst', 'sgt0', 'sgt1'] + [f'st{i}' for i in range(8)])
        proj_phase()
        T.res['A2all'] = [None, {}]
        for nm in ['kst', 'vst']:
            pass

        def store_tok(name):
            return [(name, T.dsem[name][1]), {}]

        def attn_phase():
            for i_ in range(2):
                T.res[f'KVQ{i_}'] = store_tok(f'kst{i_}')
            T.res['Vall'] = store_tok('vst')
            with ExitStack() as es:
                ktb = [sbt(es, f"ktb{i}", [128, NALL], BF16) for i in range(2)]
                vb = [sbt(es, f"vb{i}", [128, NKT, 128], BF16) for i in range(2)]
                qtb = [sbt(es, f"qtb{i}", [128, NO], BF16) for i in range(2)]
                NPT = 6
                pT = [sbt(es, f"pT{i}", [128, 1024], BF16) for i in range(NPT)]
                pa = [sbt(es, f"pa{i}", [128, 1024], BF16) for i in range(2)]
                pq = [sbt(es, f"pq{i}", [128, 1024], BF16) for i in range(2)]
                po = [sbt(es, f"po{i}", [128, 1024], BF16) for i in range(2)]
                rr = [sbt(es, f"rr{i}", [128, 512], F32) for i in range(2)]
                oo = [sbt(es, f"oo{i}", [128, 512], F32) for i in range(2)]
                dd = sbt(es, "dd", [128, 512], F32)
                sq = sbt(es, "sq", [128, 512], BF16)
                sd = sbt(es, "sd", [128, 512], F32)
                ost = [sbt(es, f"ost{i}", [128, 512], BF16) for i in range(2)]
                Vv = V.rearrange("(kt p) (h e) -> h p kt e", p=128, e=128)
                PO = [P[2][:, 0:512], P[2][:, 512:1024]]
                PL = [P[3][:, 0:512], P[3][:, 512:1024]]
                NO8 = NKT // 8
                NTAILP = (NKT - 8 * NO8) // 2
                NLM = NO8 + NTAILP
                st_ = {'gcnt': 0, 'ocnt': 0}

                def load_pair(pr):
                    b = pr % 2
                    T.dma('sp', f'kv{b}', ['KVQ0', 'KVQ1', 'Vall'], [f'ktb{b}', f'vb{b}', f'qtb{b}'],
                          [(ktb[b][:], KT[pr]), (qtb[b][:], QT[pr])] +
                          [(vb[b][:, k0:min(k0 + 8, NKT), :], Vv[pr][:, k0:min(k0 + 8, NKT), :]) for k0 in range(0, NKT, 8)])

                def epi_A():
                    for sub in range(2):
                        T.op('act', ['PL'], [f'rr{sub}'], lambda sub=sub: nc.scalar.activation(out=rr[sub][:], in_=PL[sub], func=AF.Ln))
                    for sub in range(2):
                        T.op('dve', ['PO'], [f'oo{sub}'], lambda sub=sub: nc.vector.tensor_copy(oo[sub][:], PO[sub]))

                def epi_B():
                    for sub in range(2):
                        T.op('act', [f'rr{sub}'], [f'rr{sub}'], lambda sub=sub: nc.scalar.activation(
                            out=rr[sub][:], in_=rr[sub][:], func=AF.Exp, scale=-1.0))
                    for sub in range(2):
                        T.op('dve', [f'oo{sub}', f'rr{sub}'], [f'oo{sub}'], lambda sub=sub: nc.vector.tensor_tensor(
                            out=oo[sub][:], in0=oo[sub][:], in1=rr[sub][:], op=ALU.mult))
                    T.op('dve', ['oo0', 'oo1', 'neglam'], ['dd'], lambda: nc.vector.scalar_tensor_tensor(
                        out=dd[:], in0=oo[1][:], scalar=lamt[:, 5:6], in1=oo[0][:], op0=ALU.mult, op1=ALU.add))
                    T.op('dve', ['dd'], ['sq'], lambda: nc.vector.tensor_tensor(out=sq[:], in0=dd[:], in1=dd[:], op=ALU.mult))

                def epi_C():
                    T.op('pe', ['sq', 'onesb'], ['PL'], lambda: nc.tensor.matmul(PL[0], onesb[:], sq[:], start=True, stop=True))

                def epi_D(pr, qt):
                    T.op('act', ['PL', 'epsc'], ['sd'], lambda: nc.scalar.activation(
                        out=sd[:], in_=PL[0], func=AF.Ln, scale=1.0 / 128, bias=epsc[:, 0:1]))
                    T.op('act', ['sd'], ['sd'], lambda: nc.scalar.activation(out=sd[:], in_=sd[:], func=AF.Exp, scale=-0.5))
                    ob = st_['ocnt'] % 2; st_['ocnt'] += 1
                    T.op('dve', ['sd', 'dd', 'sublc'], [f'ost{ob}'], lambda: nc.vector.scalar_tensor_tensor(
                        out=ost[ob][:], in0=dd[:], scalar=lamt[:, 6:7], in1=sd[:], op0=ALU.mult, op1=ALU.mult))
                    T.dma('sp', f'ost{ob}', [f'ost{ob}'], [f'OT{pr}_{qt}'], [(OT[:, pr, qt * 512:(qt + 1) * 512], ost[ob][:])])

                prev = None
                load_pair(0)
                for pr in range(8):
                    b = pr % 2
                    if pr + 1 < 8:
                        load_pair(pr + 1)
                    for qt in range(NTO):
                        g0 = st_['gcnt']

                        def qk(kt):
                            sb_ = (g0 + kt) % 2
                            def f():
                                ins = None
                                for sub in range(2):
                                    rs = slice(sub * 64, (sub + 1) * 64)
                                    ins = nc.tensor.matmul(P[sb_][:, sub * 512:(sub + 1) * 512], ktb[b][rs, kt * 128:(kt + 1) * 128],
                                                           qtb[b][rs, qt * 512:(qt + 1) * 512], start=True, stop=True)
                                return ins
                            T.op('pe', [f'ktb{b}', f'qtb{b}'], [f'P{sb_}'], f)

                        def ex(kt):
                            sb_ = (g0 + kt) % 2; pb = (g0 + kt) % NPT
                            T.op('act', [f'P{sb_}'], [f'pT{pb}'], lambda: nc.scalar.activation(
                                out=pT[pb][:], in_=P[sb_][:, :], func=AF.Exp, scale=0.125))

                        def pv(kt):
                            pb = (g0 + kt) % NPT
                            def f():
                                ins = None
                                for sub in range(2):
                                    ins = nc.tensor.matmul(PO[sub], vb[b][:, kt, :], pT[pb][:, sub * 512:(sub + 1) * 512],
                                                           start=(kt == 0), stop=(kt == NKT - 1))
                                return ins
                            T.op('pe', [f'vb{b}', f'pT{pb}'], ['PO'], f)

                        def padd(j):
                            p0 = (g0 + 2 * j) % NPT; p1 = (g0 + 2 * j + 1) % NPT; q = j % 2
                            T.op('dve', [f'pT{p0}', f'pT{p1}'], [f'pa{q}'], lambda: nc.vector.tensor_tensor(
                                out=pa[q][:], in0=pT[p0][:], in1=pT[p1][:], op=ALU.add))

                        def qadd(m):
                            T.op('dve', ['pa0', 'pa1'], [f'pq{m % 2}'], lambda: nc.vector.tensor_tensor(
                                out=pq[m % 2][:], in0=pa[0][:], in1=pa[1][:], op=ALU.add))

                        def lmm(src, key, idx):
                            def f():
                                ins = None
                                for sub in range(2):
                                    ins = nc.tensor.matmul(PL[sub], onesb[:], src[:, sub * 512:(sub + 1) * 512],
                                                           start=(idx == 0), stop=(idx == NLM - 1))
                                return ins
                            T.op('pe', [key, 'onesb'], ['PL'], f)
                        qk(0)
                        qk(1)
                        for kt in range(NKT):
                            ex(kt)
                            if kt + 2 < NKT:
                                qk(kt + 2)
                            pv(kt)
                            if prev is not None:
                                if kt == 0:
                                    epi_B()
                                elif kt == 2:
                                    epi_C()
                                elif kt == 3:
                                    epi_D(*prev)
                            if kt % 2 == 1 and kt < 8 * NO8:
                                padd(kt // 2)
                            if kt % 4 == 3 and kt < 8 * NO8:
                                qadd(kt // 4)
                            if kt % 8 == 7 and kt < 8 * NO8:
                                o = kt // 8
                                T.op('dve', ['pq0', 'pq1'], [f'po{o % 2}'], lambda o=o: nc.vector.tensor_tensor(
                                    out=po[o % 2][:], in0=pq[0][:], in1=pq[1][:], op=ALU.add))
                            if kt % 8 == 1 and kt >= 9 and kt - 2 < 8 * NO8:
                                o = kt // 8 - 1
                                lmm(po[o % 2], f'po{o % 2}', o)
                        if NO8 >= 1 and (8 * NO8 - 1 + 2) >= NKT:
                            o = NO8 - 1
                            lmm(po[o % 2], f'po{o % 2}', o)
                        for tp in range(NTAILP):
                            j = 4 * NO8 + tp
                            p0 = (g0 + 2 * j) % NPT; p1 = (g0 + 2 * j + 1) % NPT
                            T.op('dve', [f'pT{p0}', f'pT{p1}'], [f'pa{tp % 2}'], lambda p0=p0, p1=p1, tp=tp: nc.vector.tensor_tensor(
                                out=pa[tp % 2][:], in0=pT[p0][:], in1=pT[p1][:], op=ALU.add))
                            lmm(pa[tp % 2], f'pa{tp % 2}', NO8 + tp)
                        st_['gcnt'] += NKT
                        epi_A()
                        prev = (pr, qt)
                epi_B(); epi_C(); epi_D(*prev)
                phase_end(['ktb0', 'ktb1', 'vb0', 'vb1', 'qtb0', 'qtb1'] + [f'pT{i}' for i in range(NPT)] + ['rr0', 'rr1', 'oo0', 'oo1',
                           'dd', 'sq', 'sd', 'ost0', 'ost1', 'PO', 'PL', 'pa0', 'pa1', 'pq0', 'pq1', 'po0', 'po1'])

        dg_es = ExitStack()
        cw = sbt(dg_es, "cw", [128, 8, CONV_K], F32)
        dg = sbt(dg_es, "dg", [128, 8 * CONV_K, 128], BF16)
        T.dma('sp', 'ld_cw', [], ['cw'], [(cw[:], conv_w[:, :, :])])
        for c8 in range(8):
            def bld(c8=c8):
                ins = None
                for j in range(CONV_K):
                    ins = nc.gpsimd.tensor_scalar(out=dg[:, c8 * CONV_K + j, :], in0=ident[:], scalar1=cw[:, c8, j:j + 1],
                                                  scalar2=None, op0=ALU.mult)
                return ins
            T.op('pool', ['cw', 'ident'], [f'dg{c8}'], bld)
        attn_phase()

        def merge_phase():
            TW = 256
            NTL = NO // TW
            for i_ in range(2):
                T.res[f'OTall{i_}'] = store_tok(f'ost{i_}')
            with ExitStack() as es:
                wap = sbt(es, "wap", [128, 8, D], BF16)
                wcp = sbt(es, "wcp", [128, 8, D], BF16)
                wo = sbt(es, "wo", [128, 8, D], BF16)
                tabs = sbt(es, "tabs3", [128, D], F32)
                ub = [sbt(es, f"ub{i}", [128, 8, TW + 32], BF16) for i in range(2)]
                sga = [sbt(es, f"sga{i}", [128, 8, TW], BF16) for i in range(2)]
                ot = [sbt(es, f"ot{i}", [128, 8, TW], BF16) for i in range(2)]
                sgb = sbt(es, "sgb", [128, 8, TW], BF16)
                vv = [sbt(es, f"vv{i}", [128, 8, TW], F32) for i in range(2)]
                sqv = [sbt(es, f"sqv{i}", [128, TW], F32) for i in range(2)]
                z2 = [sbt(es, f"z2{i}", [128, 8, TW], BF16) for i in range(2)]
                mean = sbt(es, "mean", [128, TW], F32)
                msq = sbt(es, "msq", [128, TW], F32)
                rstd = sbt(es, "rstd", [128, TW], F32)
                mr = sbt(es, "mr", [128, TW], F32)
                tt = [sbt(es, f"tt{i}", [128, TW], F32) for i in range(2)]
                cT = sbt(es, "cT", [128, 8, TW], BF16)
                z1 = [sbt(es, f"z1{i}", [128, TW], F32) for i in range(2)]
                zT = sbt(es, "zT", [128, 8, TW], BF16)
                xb = [sbt(es, f"xb{i}", [128, D], F32) for i in range(2)]
                junk = sbt(es, "junk", [128, D], BF16)
                st = sbt(es, "st", [128, 8], F32)
                yb = sbt(es, "yb", [128, D], F32)
                for (wt, wd, nm) in ((wap, w_ap, 'wap'), (wcp, w_cp, 'wcp'), (wo, w_o, 'wo')):
                    T.dma('pool', 'wts_' + nm, [], [nm], [(wt[:, c, :], kc(wd)[:, c, :]) for c in range(8)])
                T.dma('sp', 'tabs', [], ['tabs'], [(tabs[:], TAB[5])])
                stt_ = {'xcnt': 0}

                def load_in(i):
                    t0 = i * TW; q = i % 2
                    T.dma('sp', f'c1in{q}', ['KVQ0', 'KVQ1', 'OTall0', 'OTall1'], [f'ub{q}', f'sga{q}', f'ot{q}'],
                          [(ub[q][:, :, 0:TW + 30], U[:, :, 512 + t0 - 15: 512 + t0 + TW + 15]),
                           (sga[q][:], SGA[:, :, t0:t0 + TW]), (ot[q][:], OT[:, :, t0:t0 + TW])])

                def ln_head(i):
                    T.op('dve', ['P2'], ['mean'], lambda: nc.vector.tensor_scalar(
                        out=mean[:], in0=P[2][:, 0:TW], scalar1=1.0 / D, scalar2=None, op0=ALU.mult))
                    T.op('dve', ['mean'], ['msq'], lambda: nc.vector.tensor_tensor(out=msq[:], in0=mean[:], in1=mean[:], op=ALU.mult))
                    T.op('dve', ['P2', 'msq'], ['rstd'], lambda: nc.vector.scalar_tensor_tensor(
                        out=rstd[:], in0=P[2][:, 512:512 + TW], scalar=1.0 / D, in1=msq[:], op0=ALU.mult, op1=ALU.subtract))
                    T.op('act', ['rstd', 'epsc'], ['rstd'], lambda: nc.scalar.activation(
                        out=rstd[:], in_=rstd[:], func=AF.Ln, scale=1.0, bias=epsc[:, 0:1]))
                    T.op('act', ['rstd'], ['rstd'], lambda: nc.scalar.activation(out=rstd[:], in_=rstd[:], func=AF.Exp, scale=-0.5))
                    T.op('dve', ['mean', 'rstd'], ['mr'], lambda: nc.vector.tensor_tensor(out=mr[:], in0=mean[:], in1=rstd[:], op=ALU.mult))

                def ln_chunk(i, c8):
                    q = i % 2; j = c8 % 2
                    T.op('dve', [f'vv{q}_{c8}', 'rstd'], [f'tt{j}'], lambda: nc.vector.tensor_tensor(
                        out=tt[j][:], in0=vv[q][:, c8, :], in1=rstd[:], op=ALU.mult))
                    T.op('dve', [f'tt{j}', 'mr'], [f'tt{j}'], lambda: nc.vector.tensor_tensor(
                        out=tt[j][:], in0=tt[j][:], in1=mr[:], op=ALU.subtract))
                    T.op('act', [f'tt{j}', 'cols'], ['cT'], lambda: nc.scalar.activation(
                        out=cT[:, c8, :], in_=tt[j][:], func=AF.Silu, scale=cols[:, 8 + c8:9 + c8], bias=cols[:, 16 + c8:17 + c8]))

                def conv_chunk(i, c8):
                    q = i % 2; j = c8 % 2
                    def cmm():
                        ins = None
                        for k in range(CONV_K):
                            ins = nc.tensor.matmul(P[j][:, 0:TW], dg[:, c8 * CONV_K + k, :], ub[q][:, c8, k:k + TW],
                                                   start=(k == 0), stop=(k == CONV_K - 1))
                        return ins
                    T.op('pe', [f'dg{c8}', f'ub{q}'], [f'P{j}a'], cmm)
                    T.op('act', [f'P{j}a', 'cols'], [f'vv{q}_{c8}'], lambda: nc.scalar.activation(
                        out=vv[q][:, c8, :], in_=P[j][:, 0:TW], func=AF.Identity, bias=cols[:, c8:c8 + 1], scale=1.0))
                    T.op('dve', [f'vv{q}_{c8}'], [f'sqv{j}'], lambda: nc.vector.tensor_tensor(
                        out=sqv[j][:], in0=vv[q][:, c8, :], in1=vv[q][:, c8, :], op=ALU.mult))
                    def amm():
                        ins = None
                        for c in range(8):
                            ins = nc.tensor.matmul(P[j][:, 512:512 + TW], wap[:, c, c8 * 128:(c8 + 1) * 128], ot[q][:, c, :],
                                                   start=(c == 0), stop=(c == 7))
                        return ins
                    T.op('pe', ['wap', f'ot{q}'], [f'P{j}b'], amm)
                    T.op('dve', [f'P{j}b', f'sga{q}', f'vv{q}_{c8}'], [f'z2{q}'], lambda: nc.vector.tensor_tensor(
                        out=z2[q][:, c8, :], in0=P[j][:, 512:512 + TW], in1=sga[q][:, c8, :], op=ALU.mult))
                    def smm():
                        nc.tensor.matmul(P[2][:, 0:TW], onesf[:], vv[q][:, c8, :], start=(c8 == 0), stop=(c8 == 7))
                        return nc.tensor.matmul(P[2][:, 512:512 + TW], onesf[:], sqv[j][:], start=(c8 == 0), stop=(c8 == 7))
                    T.op('pe', [f'vv{q}_{c8}', f'sqv{j}', 'onesf'], ['P2'], smm)

                def out_stage(i):
                    t0 = i * TW; q = i % 2
                    T.dma('sp', 'sgb', ['KVQ0', 'KVQ1'], ['sgb'], [(sgb[:], SGB[:, :, t0:t0 + TW])])
                    for d8 in range(8):
                        j = d8 % 2
                        def bmm(d8=d8, j=j):
                            ins = None
                            for c in range(8):
                                ins = nc.tensor.matmul(P[j][:, 0:TW], wcp[:, c, d8 * 128:(d8 + 1) * 128], cT[:, c, :], start=(c == 0), stop=(c == 7))
                            return ins
                        T.op('pe', ['wcp', 'cT'], [f'P{j}a'], bmm)
                        T.op('dve', [f'P{j}a', 'sgb'], [f'z1{j}'], lambda d8=d8, j=j: nc.vector.tensor_tensor(
                            out=z1[j][:], in0=P[j][:, 0:TW], in1=sgb[:, d8, :], op=ALU.mult))
                        T.op('dve', [f'z1{j}', f'z2{q}'], ['zT'], lambda d8=d8, j=j: nc.vector.tensor_tensor(
                            out=zT[:, d8, :], in0=z1[j][:], in1=z2[q][:, d8, :], op=ALU.add))
                    for s in range(TW // 128):
                        def omm(s=s):
                            ins = None
                            for n in range(2):
                                for c in range(8):
                                    ins = nc.tensor.matmul(P[3][:, n * 512:(n + 1) * 512], zT[:, c, s * 128:(s + 1) * 128],
                                                           wo[:, c, n * 512:(n + 1) * 512], start=(c == 0), stop=(c == 7))
                            return ins
                        T.op('pe', ['wo', 'zT'], ['P3'], omm)
                        b = stt_['xcnt'] % 2; stt_['xcnt'] += 1
                        T.dma('sp', f'x{b}', ['X1all0', 'X1all1'], [f'xb{b}'], [(xb[b][:], X1[t0 + s * 128: t0 + (s + 1) * 128, :])])
                        c0 = (s % 2) * 4
                        T.op('act', ['P3'], ['yb'], lambda: nc.scalar.copy(out=yb[:], in_=P[3][:, :]))
                        T.op('act', ['yb'], ['junk', f'st{c0}'], lambda c0=c0: nc.scalar.activation(
                            out=junk[:], in_=yb[:], func=AF.Square, accum_out=st[:, c0:c0 + 1]))
                        T.op('act', [f'st{c0}', 'epsc'], [f'st{c0 + 1}'], lambda c0=c0: nc.scalar.activation(
                            out=st[:, c0 + 1:c0 + 2], in_=st[:, c0:c0 + 1], func=AF.Ln, scale=1.0 / D, bias=epsc[:, 0:1]))
                        T.op('act', [f'st{c0 + 1}'], [f'st{c0 + 2}'], lambda c0=c0: nc.scalar.activation(
                            out=st[:, c0 + 2:c0 + 3], in_=st[:, c0 + 1:c0 + 2], func=AF.Exp, scale=-0.5))
                        T.op('dve', ['yb', f'st{c0 + 2}', 'tabs'], ['yb'], lambda c0=c0: nc.vector.scalar_tensor_tensor(
                            out=yb[:], in0=yb[:], scalar=st[:, c0 + 2:c0 + 3], in1=tabs[:], op0=ALU.mult, op1=ALU.mult))
                        T.op('dve', ['yb', f'xb{b}'], [f'xb{b}'], lambda b=b: nc.vector.tensor_tensor(
                            out=xb[b][:], in0=yb[:], in1=xb[b][:], op=ALU.add))
                        T.dma('pool', f'xst{b}', [f'xb{b}'], ['X2' + str(t0 + s * 128)], [(X2[t0 + s * 128: t0 + (s + 1) * 128, :], xb[b][:])])

                load_in(0)
                if NTL > 1:
                    load_in(1)
                for c8 in range(8):
                    conv_chunk(0, c8)
                ln_head(0)
                for i in range(NTL):
                    if i + 1 < NTL:
                        for c8 in range(8):
                            ln_chunk(i, c8)
                            conv_chunk(i + 1, c8)
                        ln_head(i + 1)
                        if i + 2 < NTL:
                            load_in(i + 2)
                    else:
                        for c8 in range(8):
                            ln_chunk(i, c8)
                    out_stage(i)
                phase_end(['wap', 'wcp', 'wo', 'cw', 'tabs', 'ub0', 'ub1', 'sga0', 'sga1', 'ot0', 'ot1', 'sgb', 'sqv0', 'sqv1', 'mean', 'msq',
                           'rstd', 'mr', 'tt0', 'tt1', 'cT', 'z10', 'z11', 'z20', 'z21', 'zT', 'xb0', 'xb1', 'junk', 'yb',
                           'P0a', 'P0b', 'P1a', 'P1b', 'P3']
                          + [f'vv{q}_{i}' for i in range(8) for q in range(2)] + [f'st{i}' for i in range(8)])
        merge_phase()
        for e in ['pe', 'act', 'dve', 'pool', 'sp']:
            T.wait_all(e, [f'dg{c8}' for c8 in range(8)] + ['cw'])
        dg_es.close()
        for i_ in range(2):
            T.res[f'X2all{i_}'] = store_tok(f'xst{i_}')
        for e in ['sp']:
            T.wait_all(e, ['X2all0', 'X2all1'])
        ffn_phase(1, X2, out, (6, 7, 8), (6, 7, 8), [(i * 512, 4, False) for i in range(NTO)])
        for i_ in range(2):
            nc.sync.wait_ge(T.dsem[f'xst{i_}'][0], T.dsem[f'xst{i_}'][1])
            nc.gpsimd.wait_ge(T.dsem[f'xst{i_}'][0], T.dsem[f'xst{i_}'][1])

        if dbg:
            for nm in ['xst0', 'xst1', 'kst0', 'kst1', 'vst', 'ost0', 'ost1']:
                nc.sync.wait_ge(T.dsem[nm][0], T.dsem[nm][1])
    return nc


def _host_prep(inputs, NO, n_cores, batch_of, half_of):
    x = np.asarray(inputs['x'], np.float32)
    ctx = np.asarray(inputs['ctx'], np.float32)
    c = np.asarray(inputs['c'], np.float32)
    c_ctx = np.asarray(inputs['c_ctx'], np.float32)
    NTOK = 2 * NO
    NALL = NTOK + CTX
    w_in = np.ascontiguousarray(np.asarray(inputs['w_in'], np.float32)[0])
    n_rows = NTOK // GRID_W
    pos = np.stack([np.repeat(np.arange(n_rows), GRID_W), np.tile(np.arange(GRID_W), n_rows)], -1).astype(np.float32)
    inv_freq = (10000.0 ** (-np.arange(0, 32, 2, dtype=np.float32) / 32)).astype(np.float32)
    ang = pos[:, :, None] * inv_freq
    cos, sin = np.cos(ang).astype(np.float32), np.sin(ang).astype(np.float32)
    dd = np.arange(128) % 64
    a_i, h_i, j_i = dd // 32, (dd % 32) // 16, dd % 16
    cosF = cos[:, a_i, j_i].T
    sinF = (sin[:, a_i, j_i] * np.where(h_i == 0, -1.0, 1.0)[None, :]).T
    colv_common = np.zeros((128, 40), np.float32)
    for i, nm in enumerate(['conv_b', 'conv_ln_g', 'conv_ln_b']):
        colv_common[:, i * 8:(i + 1) * 8] = np.asarray(inputs[nm], np.float32)[0].reshape(8, 128).T
    colv_common[:, 24] = np.asarray(inputs['subln_g'], np.float32)[0]
    conv_w = np.ascontiguousarray(np.asarray(inputs['conv_w'], np.float32)[0].reshape(CONV_K, 8, 128).transpose(2, 1, 0))
    shared = {
        'w_ada': np.ascontiguousarray(np.asarray(inputs['w_ada'], np.float32)[0]),
        'b_ada': np.ascontiguousarray(np.asarray(inputs['b_ada'], np.float32)[0][None, :]),
        'norm_g': np.ascontiguousarray(np.asarray(inputs['norm_g'], np.float32)[0]),
        'ffn1_w13': np.ascontiguousarray(np.asarray(inputs['ffn1_w13'], np.float32)[0]),
        'ffn2_w13': np.ascontiguousarray(np.asarray(inputs['ffn2_w13'], np.float32)[0]),
        'ffn1_w2': np.ascontiguousarray(np.asarray(inputs['ffn1_w2'], np.float32)[0]),
        'ffn2_w2': np.ascontiguousarray(np.asarray(inputs['ffn2_w2'], np.float32)[0]),
        'w_in': w_in,
        'lambda_qk': np.ascontiguousarray(np.asarray(inputs['lambda_qk'], np.float32)[0].reshape(1, 256)),
        'conv_w': conv_w,
        'w_attn_proj': np.ascontiguousarray(np.asarray(inputs['w_attn_proj'], np.float32)[0]),
        'w_conv_proj': np.ascontiguousarray(np.asarray(inputs['w_conv_proj'], np.float32)[0]),
        'w_out': np.ascontiguousarray(np.asarray(inputs['w_out'], np.float32)[0]),
    }
    maps = []
    for ci in range(n_cores):
        b, h = batch_of(ci), half_of(ci)
        own = slice(h * NO, (h + 1) * NO)
        oth = slice((1 - h) * NO, (2 - h) * NO)
        m = dict(shared)
        m['xc'] = np.concatenate([x[b, own], x[b, oth], ctx[b]], 0)
        cv = np.zeros((128, 16), np.float32)
        cv[:, 0:8] = c[b].reshape(8, 128).T
        cv[:, 8:16] = c_ctx.reshape(8, 128).T
        m['cvec'] = cv
        colv = colv_common.copy()
        colv[:, 25] = 1.0 if h == 1 else 0.0
        colv[:, 26] = 1.0 if h == 0 else 0.0
        m['colv'] = colv
        ct = np.ones((128, NALL), np.float32); sn = np.zeros((128, NALL), np.float32)
        ct[:, :NO] = cosF[:, own]; ct[:, NO:2 * NO] = cosF[:, oth]
        sn[:, :NO] = sinF[:, own]; sn[:, NO:2 * NO] = sinF[:, oth]
        m['cosT'] = ct; m['sinT'] = sn
        maps.append(m)
    return maps


_NC_CACHE = {}


def kernel(**inputs):
    NO = 4096
    if NO not in _NC_CACHE:
        _NC_CACHE[NO] = build(NO)
    nc = _NC_CACHE[NO]
    maps = _host_prep(inputs, NO, 8, lambda ci: ci // 2, lambda ci: ci % 2)
    res = run_bass_kernel_spmd(nc, maps, core_ids=list(range(8)))
    outp = np.empty((4, 2 * NO, D), np.float32)
    for ci in range(8):
        outp[ci // 2, (ci % 2) * NO:(ci % 2 + 1) * NO] = res.results[ci]["out"]
    return outp
```
